# Optimizing a Trainium2 kernel written in Bass

```python
import jax, jax.numpy as jnp
from jax import lax
import numpy as np

D_MODEL = 1024
BATCH = 16
SEQ = 2048
DEPTH = 2
DEC_BATCH = 32
DEC_SEQ = 4
PAST_LEN = 16384
PAGE_SIZE = 128

CHUNK = 128
A_GROUPS = 8
A_WIDTH = D_MODEL
A_GROUP_DIM = A_WIDTH // A_GROUPS
N_HEADS = 8
HEAD_DIM = 128
N_KV_HEADS = 2
KV_GROUP = N_HEADS // N_KV_HEADS
IDX_HEADS = 8
IDX_DIM = 64
TOPK_MAX = 256
Q_BLOCK = 128
D_FF = 2816
EPS = 1e-6

SPLIT_SIZES = (A_WIDTH, A_WIDTH, N_HEADS * HEAD_DIM, N_KV_HEADS * HEAD_DIM, N_KV_HEADS * HEAD_DIM,
               IDX_HEADS * IDX_DIM, IDX_DIM, IDX_HEADS, D_MODEL, D_MODEL)
D_IN = sum(SPLIT_SIZES)
SPLIT_POINTS = tuple(int(c) for c in np.cumsum(SPLIT_SIZES)[:-1])

kernel_name = 'hybrid_gmlp_dsa_macaron_step'


def _rmsnorm(x, g):
    x32 = x.astype(jnp.float32)
    y = x32 * lax.rsqrt(jnp.mean(x32 * x32, axis=-1, keepdims=True) + EPS)
    return y.astype(x.dtype) * g


def _swiglu(x, w_up, w_down):
    gate, up = jnp.split(x @ w_up, 2, axis=-1)
    return (jax.nn.silu(gate) * up) @ w_down


_gather_rows = jax.vmap(lambda a, i: a[i])


def _chunk_spatial(vc, w_s, b_s):
    n = vc.shape[2]
    w = jnp.where(jnp.tril(jnp.ones((n, n), dtype=bool)), w_s[:, :n, :n], 0)
    return jnp.einsum('gts,bnsgc->bntgc', w, vc) + b_s[:, :n].T[None, None, :, :, None]


def _index_scores(qi, wi, ki):
    dots = jnp.einsum('bthd,bsd->bths', qi.astype(jnp.float32), ki.astype(jnp.float32)) * IDX_DIM ** -0.5
    return jnp.einsum('bth,bths->bts', wi.astype(jnp.float32), jax.nn.relu(dots))


def _sparse_attend(q, kg, vg, valid):
    B, T = q.shape[:2]
    qg = q.reshape(B, T, N_KV_HEADS, KV_GROUP, HEAD_DIM).astype(jnp.float32)
    s = jnp.einsum('btkgd,btskd->btkgs', qg, kg.astype(jnp.float32)) * HEAD_DIM ** -0.5
    s = jnp.where(valid[:, :, None, None, :], s, -jnp.inf)
    p = jax.nn.softmax(s, axis=-1)
    o = jnp.einsum('btkgs,btskd->btkgd', p, vg.astype(jnp.float32))
    return o.reshape(B, T, N_HEADS * HEAD_DIM).astype(q.dtype)


def _prompt_mixer(w_s, b_s):
    def mix(v, q, k, vv, qi, ki, wi):
        B, S = v.shape[:2]
        vc = v.reshape(B, S // CHUNK, CHUNK, A_GROUPS, A_GROUP_DIM)
        sv = _chunk_spatial(vc, w_s, b_s).reshape(B, S, A_WIDTH)
        k_sel = min(TOPK_MAX, S // 4)
        key_pos = jnp.arange(S)

        def block(n):
            t0 = n * Q_BLOCK
            qb = lax.dynamic_slice_in_dim(q, t0, Q_BLOCK, axis=1)
            qib = lax.dynamic_slice_in_dim(qi, t0, Q_BLOCK, axis=1)
            wib = lax.dynamic_slice_in_dim(wi, t0, Q_BLOCK, axis=1)
            q_pos = t0 + jnp.arange(Q_BLOCK)
            causal = key_pos[None, None, :] <= q_pos[None, :, None]
            scores = jnp.where(causal, _index_scores(qib, wib, ki), -jnp.inf)
            _, idx = lax.top_k(scores, k_sel)
            valid = idx <= q_pos[None, :, None]
            return _sparse_attend(qb, _gather_rows(k, idx), _gather_rows(vv, idx), valid)

        out = lax.map(block, jnp.arange(S // Q_BLOCK))
        attn = out.transpose(1, 0, 2, 3).reshape(B, S, N_HEADS * HEAD_DIM)
        return sv, attn
    return mix


def _sample_mixer(w_s, b_s, cache_k, cache_v, cache_kidx, page_table):
    def mix(v, q, k, vv, qi, ki, wi):
        DB, T = v.shape[:2]
        sv = _chunk_spatial(v.reshape(DB, 1, T, A_GROUPS, A_GROUP_DIM), w_s, b_s).reshape(DB, T, A_WIDTH)
        past = page_table.shape[1] * PAGE_SIZE
        L = past + T
        k_sel = min(TOPK_MAX, L // 4)
        ki_past = cache_kidx[page_table].reshape(DB, past, IDX_DIM)
        ki_all = jnp.concatenate([ki_past, ki.astype(ki_past.dtype)], axis=1)
        q_pos = past + jnp.arange(T)
        causal = jnp.arange(L)[None, None, :] <= q_pos[None, :, None]
        scores = jnp.where(causal, _index_scores(qi, wi, ki_all), -jnp.inf)
        _, idx = lax.top_k(scores, k_sel)
        pidx = jnp.minimum(idx, past - 1)
        phys = _gather_rows(page_table, pidx // PAGE_SIZE)
        off = pidx % PAGE_SIZE
        nidx = jnp.clip(idx - past, 0, T - 1)
        from_past = (idx < past)[..., None, None]
        kg = jnp.where(from_past, cache_k[phys, off], _gather_rows(k, nidx))
        vg = jnp.where(from_past, cache_v[phys, off], _gather_rows(vv, nidx))
        valid = idx <= q_pos[None, :, None]
        return sv, _sparse_attend(q, kg, vg, valid)
    return mix


def _layer(x, p, mixer):
    B, T = x.shape[:2]
    x = x + 0.5 * _swiglu(_rmsnorm(x, p['g_ffn1']), p['w_up1'], p['w_down1'])
    h = _rmsnorm(x, p['g_mix'])
    u, v, q, k, vv, qi, ki, wi, ga, gb = jnp.split(h @ p['w_in'], SPLIT_POINTS, axis=-1)
    u = jax.nn.gelu(u)
    v = _rmsnorm(jax.nn.gelu(v), p['g_v'])
    q = _rmsnorm(q.reshape(B, T, N_HEADS, HEAD_DIM), p['g_q'])
    k = _rmsnorm(k.reshape(B, T, N_KV_HEADS, HEAD_DIM), p['g_k'])
    vv = vv.reshape(B, T, N_KV_HEADS, HEAD_DIM)
    qi = qi.reshape(B, T, IDX_HEADS, IDX_DIM)
    wi = wi * IDX_HEADS ** -0.5
    sv, attn = mixer(v, q, k, vv, qi, ki, wi)
    merged = jax.nn.sigmoid(ga) * ((u * sv) @ p['w_pa']) + jax.nn.sigmoid(gb) * (attn @ p['w_pb'])
    x = x + merged @ p['w_out']
    x = x + 0.5 * _swiglu(_rmsnorm(x, p['g_ffn2']), p['w_up2'], p['w_down2'])
    return x, k, vv, ki, v


def setup_inputs(seed: int = 0) -> dict:
    key = jax.random.key(seed)
    ks = jax.random.split(key, 26)
    n_pages = PAST_LEN // PAGE_SIZE
    n_used = DEC_BATCH * n_pages
    n_pool = n_used + n_used // 4

    def nrm(k, shape, scale):
        return jax.random.normal(k, shape, jnp.float32) * scale

    def gain(k, shape):
        return 1.0 + 0.02 * jax.random.normal(k, shape, jnp.float32)

    page_table = jax.random.permutation(ks[0], n_pool)[:n_used].reshape(DEC_BATCH, n_pages).astype(jnp.int32)
    return {
        'x_prompt': nrm(ks[1], (BATCH, SEQ, D_MODEL), 1.0),
        'x_sample': nrm(ks[2], (DEC_BATCH, DEC_SEQ, D_MODEL), 1.0),
        'cache_k': nrm(ks[3], (DEPTH, n_pool, PAGE_SIZE, N_KV_HEADS, HEAD_DIM), 1.0),
        'cache_v': nrm(ks[4], (DEPTH, n_pool, PAGE_SIZE, N_KV_HEADS, HEAD_DIM), 1.0),
        'cache_kidx': nrm(ks[5], (DEPTH, n_pool, PAGE_SIZE, IDX_DIM), 1.0),
        'page_table': page_table,
        'g_ffn1': gain(ks[6], (DEPTH, D_MODEL)),
        'w_up1': nrm(ks[7], (DEPTH, D_MODEL, 2 * D_FF), D_MODEL ** -0.5),
        'w_down1': nrm(ks[8], (DEPTH, D_FF, D_MODEL), D_FF ** -0.5),
        'g_mix': gain(ks[9], (DEPTH, D_MODEL)),
        'w_in': nrm(ks[10], (DEPTH, D_MODEL, D_IN), D_MODEL ** -0.5),
        'g_v': gain(ks[11], (DEPTH, A_WIDTH)),
        'g_q': gain(ks[12], (DEPTH, HEAD_DIM)),
        'g_k': gain(ks[13], (DEPTH, HEAD_DIM)),
        'w_s': nrm(ks[14], (DEPTH, A_GROUPS, CHUNK, CHUNK), CHUNK ** -0.5),
        'b_s': 1.0 + 0.1 * jax.random.normal(ks[15], (DEPTH, A_GROUPS, CHUNK), jnp.float32),
        'w_pa': nrm(ks[16], (DEPTH, A_WIDTH, D_MODEL), A_WIDTH ** -0.5),
        'w_pb': nrm(ks[17], (DEPTH, N_HEADS * HEAD_DIM, D_MODEL), (N_HEADS * HEAD_DIM) ** -0.5),
        'w_out': nrm(ks[18], (DEPTH, D_MODEL, D_MODEL), D_MODEL ** -0.5),
        'g_ffn2': gain(ks[19], (DEPTH, D_MODEL)),
        'w_up2': nrm(ks[20], (DEPTH, D_MODEL, 2 * D_FF), D_MODEL ** -0.5),
        'w_down2': nrm(ks[21], (DEPTH, D_FF, D_MODEL), D_FF ** -0.5),
    }


def reference(x_prompt, x_sample, cache_k, cache_v, cache_kidx, page_table,
              g_ffn1, w_up1, w_down1, g_mix, w_in, g_v, g_q, g_k, w_s, b_s,
              w_pa, w_pb, w_out, g_ffn2, w_up2, w_down2):
    xp, xs = x_prompt, x_sample
    kp_l, vp_l, kip_l, ks_l, vs_l, kis_l, vas_l = [], [], [], [], [], [], []
    for l in range(DEPTH):
        p = {'g_ffn1': g_ffn1[l], 'w_up1': w_up1[l], 'w_down1': w_down1[l], 'g_mix': g_mix[l],
             'w_in': w_in[l], 'g_v': g_v[l], 'g_q': g_q[l], 'g_k': g_k[l],
             'w_pa': w_pa[l], 'w_pb': w_pb[l], 'w_out': w_out[l],
             'g_ffn2': g_ffn2[l], 'w_up2': w_up2[l], 'w_down2': w_down2[l]}
        xp, kp, vp, kip, _ = _layer(xp, p, _prompt_mixer(w_s[l], b_s[l]))
        xs, kss, vss, kis, vas = _layer(
            xs, p, _sample_mixer(w_s[l], b_s[l], cache_k[l], cache_v[l], cache_kidx[l], page_table))
        kp_l.append(kp); vp_l.append(vp); kip_l.append(kip)
        ks_l.append(kss); vs_l.append(vss); kis_l.append(kis); vas_l.append(vas)
    return (xp, xs, jnp.stack(kp_l), jnp.stack(vp_l), jnp.stack(kip_l),
            jnp.stack(ks_l), jnp.stack(vs_l), jnp.stack(kis_l), jnp.stack(vas_l))
```

```python
from contextlib import ExitStack
import numpy as np
import concourse.bass as bass
import concourse.mybir as mybir
from concourse.bass_utils import run_bass_kernel_spmd

F32 = mybir.dt.float32
BF16 = mybir.dt.bfloat16
I32 = mybir.dt.int32
AF = mybir.ActivationFunctionType
ALU = mybir.AluOpType

import os as _os0
SAME_ENGINE_SYNC = _os0.environ.get('SES', '1') == '1'
EPOCH = 20000
NCORES = 8
SAVE_BF16 = True
import os as _os
DBG_STAGE = _os.environ.get('DBG_STAGE', '')


class Res:
    __slots__ = ("last_w", "readers", "dsem", "dcount")

    def __init__(self):
        self.last_w = None
        self.readers = []
        self.dsem = None
        self.dcount = 0


class Op:
    __slots__ = ("eng", "fn", "deps", "is_dma", "signal", "sem", "val", "key")

    def __init__(self, eng, fn, is_dma, key=None):
        self.eng = eng
        self.fn = fn
        self.deps = []
        self.is_dma = is_dma
        self.signal = False
        self.sem = None
        self.val = 0
        self.key = key


class Prog:
    ENGS = ("pe", "act", "dve", "pool", "sp")

    def __init__(self, nc):
        self.nc = nc
        self.ops = []
        self.sems = []
        self.out_dmas = []

    def new_sem(self, name):
        s = self.nc.alloc_semaphore(name=name)
        self.sems.append(s)
        return s

    def _add(self, op, reads, writes):
        deps = set()
        for r in reads:
            if r.last_w is not None:
                deps.add(r.last_w)
        for w in writes:
            if w.last_w is not None:
                deps.add(w.last_w)
            for rd in w.readers:
                deps.add(rd)
        deps.discard(op)
        op.deps = list(deps)
        for r in reads:
            r.readers.append(op)
        for w in writes:
            w.last_w = op
            w.readers = []
        self.ops.append(op)
        return op

    def op(self, eng, fn, reads=(), writes=()):
        return self._add(Op(eng, fn, False), list(reads), list(writes))

    def dma(self, eng, fn, reads=(), writes=(), key=None, is_out=False):
        reads = list(reads)
        writes = list(writes)
        if key is None:
            key = writes[0] if writes else reads[0]
        o = Op(eng, fn, True, key)
        self._add(o, reads, writes)
        o.signal = True
        if is_out:
            self.out_dmas.append(o)
        return o

    def emit(self, tag):
        nc = self.nc
        ops = self.ops

        def skip(p, o):
            return (not p.is_dma) and p.eng == o.eng and (not o.is_dma) and (
                p.eng == "pe" or not SAME_ENGINE_SYNC)

        for o in ops:
            for p in o.deps:
                if p.is_dma or skip(p, o):
                    continue
                p.signal = True
        eng_sems = {}
        eng_cnt = {}
        for o in ops:
            if o.is_dma:
                k = o.key
                if k.dsem is None:
                    k.dsem = self.new_sem("%sd%d" % (tag, len(self.sems)))
                k.dcount += 16
                o.sem = k.dsem
                o.val = k.dcount
            elif o.signal:
                c = eng_cnt.get(o.eng, 0)
                ep = c // EPOCH
                lst = eng_sems.setdefault(o.eng, [])
                if ep >= len(lst):
                    lst.append(self.new_sem("%se_%s_%d" % (tag, o.eng, ep)))
                o.sem = lst[ep]
                o.val = c - ep * EPOCH + 1
                eng_cnt[o.eng] = c + 1
        by_eng = {e: [] for e in self.ENGS}
        for o in ops:
            by_eng[o.eng].append(o)
        finals = {}
        for o in [o_ for o_ in ops if o_.is_dma]:
            k = id(o.sem)
            if k not in finals or finals[k][1] < o.val:
                finals[k] = (o.sem, o.val)

        def run(engine, lst, extra_final=None):
            waited = {}
            for o in lst:
                need = {}
                for p in o.deps:
                    if not p.signal or skip(p, o):
                        continue
                    k = id(p.sem)
                    if k not in need or need[k][1] < p.val:
                        need[k] = (p.sem, p.val)
                for k, (sem, val) in need.items():
                    if waited.get(k, 0) >= val:
                        continue
                    engine.wait_ge(sem, val)
                    waited[k] = val
                ins = o.fn(engine)
                if o.signal:
                    ins.then_inc(o.sem, 16 if o.is_dma else 1)
            if extra_final:
                for sem, val in extra_final:
                    if waited.get(id(sem), 0) < val:
                        engine.wait_ge(sem, val)

        with nc.Block() as block:
            @block.sync
            def _(e):
                run(e, by_eng["sp"], list(finals.values()))
            if by_eng["pe"]:
                @block.tensor
                def _(e):
                    run(e, by_eng["pe"])
            if by_eng["act"]:
                @block.scalar
                def _(e):
                    run(e, by_eng["act"])
            if by_eng["dve"]:
                @block.vector
                def _(e):
                    run(e, by_eng["dve"])
            if by_eng["pool"]:
                @block.gpsimd
                def _(e):
                    run(e, by_eng["pool"])
        nc.clear_and_free_semaphores(self.sems)
        nc.all_engine_barrier()


D = 1024
DC = 8
DFF = 2816
FC = 22
DIN = 6216
NH = 8
NKV = 2
HD = 128
IH = 8
ID = 64
C_U, C_V, C_Q, C_K, C_VV, C_QI, C_KI, C_WI, C_GA, C_GB = 0, 1024, 2048, 3072, 3328, 3584, 4096, 4160, 4168, 5192
EPS = 1e-6
NEG = -1.0e30
MB = -30000.0
BIS_ITERS = 22
BIS_LO = -16.0
BIS_W = 32.0


class Cfg:
    def __init__(self, nseq, S, ksel, ndec, tdec, npg, pool, depth=2, ksel_s=256):
        self.nseq, self.S, self.ksel, self.ksel_s = nseq, S, ksel, ksel_s
        self.ndec, self.tdec, self.npg, self.pool, self.depth = ndec, tdec, npg, pool, depth


FULL = Cfg(2, 2048, 256, 4, 4, 128, 5120)


class Tl:
    def __init__(self, t, nres=1):
        self.t = t
        self.r = [Res() for _ in range(nres)]


class K:
    def __init__(self, nc, P, stack, pfx=""):
        self.nc, self.P, self.stack, self.pfx = nc, P, stack, pfx
        self.wslot_i = 0

    def sb(self, name, shape, dt, nres=1):
        t = self.stack.enter_context(self.nc.sbuf_tensor(self.pfx + name, list(shape), dt))
        return Tl(t, nres)

    def ps(self, name, shape, dt=F32, nres=1):
        t = self.stack.enter_context(self.nc.psum_tensor(self.pfx + name, list(shape), dt))
        return Tl(t, nres)

    def MM(self, out, lhsT, rhs, start, stop, R, W):
        self.P.op("pe", lambda e: e.matmul(out, lhsT=lhsT, rhs=rhs, start=start, stop=stop), R, W)

    def TR(self, out, in_, ident, R, W):
        self.P.op("pe", lambda e: e.transpose(out=out, in_=in_, identity=ident), R, W)

    def ACT(self, out, in_, func, R, W, bias=None, scale=None):
        kw = {}
        if bias is not None:
            kw["bias"] = bias
        if scale is not None:
            kw["scale"] = scale
        self.P.op("act", lambda e: e.activation(out=out, in_=in_, func=func, **kw), R, W)

    def TS(self, eng, out, in0, s1, s2, op0, op1, R, W, accum_out=None):
        kw = {}
        if op1 is not None:
            kw["op1"] = op1
        if accum_out is not None:
            kw["accum_out"] = accum_out
        self.P.op(eng, lambda e: e.tensor_scalar(out=out, in0=in0, scalar1=s1, scalar2=s2, op0=op0, **kw), R, W)

    def TT(self, eng, out, in0, in1, op, R, W):
        self.P.op(eng, lambda e: e.tensor_tensor(out=out, in0=in0, in1=in1, op=op), R, W)

    def STT(self, out, in0, scalar, in1, op0, op1, R, W):
        self.P.op("dve", lambda e: e.scalar_tensor_tensor(out=out, in0=in0, scalar=scalar, in1=in1, op0=op0, op1=op1), R, W)

    def CP(self, eng, out, in_, R, W):
        if eng == "act":
            self.P.op("act", lambda e: e.copy(out=out, in_=in_), R, W)
        elif eng == "dve":
            self.P.op("dve", lambda e: e.tensor_scalar(out=out, in0=in_, scalar1=1.0, scalar2=None, op0=ALU.mult), R, W)
        else:
            self.P.op(eng, lambda e: e.tensor_copy(out=out, in_=in_), R, W)

    def MS(self, eng, ap, val, W):
        self.P.op(eng, lambda e: e.memset(ap, val), [], W)

    def DMA(self, eng, out, in_, R, W, is_out=False, key=None):
        self.P.dma(eng, lambda e: e.dma_start(out=out, in_=in_), R, W, key=key, is_out=is_out)


class Dense:
    def __init__(self, k, NT, dr, consts, wmode="cast", out_q="sp", nslots=3):
        self.k, self.NT, self.dr, self.c = k, NT, dr, consts
        self.wmode, self.out_q = wmode, out_q
        self.QITz = None
        NTT = NT // 128
        self.NTT = NTT
        k_ = k
        self.x = k_.sb("x", [128, DC, NT], F32, DC)
        self.h = k_.sb("h", [128, DC, NT], BF16, DC)
        self.sq = k_.sb("sq", [128, 2, NT], BF16, 2)
        self.rstd = k_.sb("rstd", [128, NT], F32)
        hidraw = k_.sb("hidraw", [128, FC * NT // 2], F32, FC)
        self.hidraw = hidraw
        self.hid = Tl(hidraw.t[:].bitcast(BF16).rearrange("p (c n) -> p c n", c=FC), 0)
        self.hid.r = hidraw.r
        self.tmpf = k_.sb("tmpf", [128, 3, 512], F32, 3)
        self.tmp_i = 0
        self.nslots = nslots
        self.wring = k_.sb("wring", [128, nslots, 4096], BF16, nslots)
        self.wsave = [Res() for _ in range(nslots)]
        self.banks = [k_.ps("bank%d" % i, [128, 512]) for i in range(8)]
        self.bank_i = 0
        self.eps = k_.sb("eps", [128, 1], F32)
        k_.MS("pool", self.eps.t[:], EPS, self.eps.r)
        self.QT = k_.sb("QT", [128, NH, NT], BF16, NH)
        self.QIT = k_.sb("QIT", [64, IH, NT], BF16, IH)
        self.aT = k_.sb("aT", [128, DC, NT], BF16, DC)
        self.attnT = k_.sb("attnT", [128, NH, NT], BF16, NH)
        self.mT = self.QT
        self.wtok = k_.sb("wtok", [128, NTT, IH], F32, NTT)
        self.vg = k_.sb("vg", [128, D], F32)
        self.vn = k_.sb("vn", [128, 1, D], BF16, 1)
        self.vnf = k_.sb("vnf", [128, D], F32) if NT == 128 else None
        self.vjunk = k_.sb("vjunk", [128, D], BF16)
        self.sm = k_.sb("sm", [128, 8], F32, 4)
        self.vstage = k_.sb("vstage", [128, 1, NKV * HD], F32, 1)
        self.kstage = k_.sb("kstage", [128, 1, NT], F32, 1)

    def bank(self, group=(0, 1, 2, 3)):
        b = self.banks[group[self.bank_i % len(group)]]
        self.bank_i += 1
        return b

    def tmp(self):
        i = self.tmp_i % 3
        self.tmp_i += 1
        return self.tmpf.t[:, i, 0:self.NT], self.tmpf.r[i]

    def wload(self, w2d, KC, c0, nc_):
        k = self.k
        i = k.wslot_i % self.nslots
        k.wslot_i += 1
        view = self.wring.t[:, i, 0:KC * nc_].rearrange("p (k n) -> p k n", k=KC)
        src = w2d.rearrange("(k p) n -> p k n", p=128)[:, :, c0:c0 + nc_]
        kk, nn = w2d.shape
        L = w2d.offset // (kk * nn)
        bf = self.dr[w2d.tensor.name + "_bf"][L].rearrange("(k p) n -> p k n", p=128)[:, :, c0:c0 + nc_]
        wr = self.wring.r[i]
        parts = [(0, KC)] if KC <= 8 else [(0, KC // 2), (KC // 2, KC)]
        for (a0, a1) in parts:
            if self.wmode == "bf16":
                k.DMA("sp", view[:, a0:a1, :], bf[:, a0:a1, :], [], [wr])
            else:
                k.DMA("pool", view[:, a0:a1, :], src[:, a0:a1, :], [], [wr])
        if self.wmode == "save":
            for (a0, a1) in parts:
                k.DMA("sp", bf[:, a0:a1, :], view[:, a0:a1, :], [wr], [], is_out=True, key=self.wsave[i])
        return view, self.wring.r[i]

    def rmsnorm(self, gT):
        k, NT = self.k, self.NT
        st = self.bank((4,))
        for c in range(DC):
            j = c % 2
            k.ACT(self.sq.t[:, j, :], self.x.t[:, c, :], AF.Square, [self.x.r[c]], [self.sq.r[j]])
            k.MM(st.t[:, 0:NT], self.c["ones"].t[:], self.sq.t[:, j, :], c == 0, c == DC - 1, [self.sq.r[j], self.c["ones"].r[0]], st.r)
        k.ACT(self.rstd.t[:], st.t[:, 0:NT], AF.Ln, st.r + self.eps.r, self.rstd.r, bias=self.eps.t[:], scale=1.0 / D)
        k.ACT(self.rstd.t[:], self.rstd.t[:], AF.Exp, self.rstd.r, self.rstd.r, scale=-0.5)
        for c in range(DC):
            k.STT(self.h.t[:, c, :], self.x.t[:, c, :], gT.t[:, c:c + 1], self.rstd.t[:], ALU.mult, ALU.mult,
                  [self.x.r[c], gT.r[0], self.rstd.r[0]], [self.h.r[c]])

    def proj(self, wview, wres, col, M, src, KC, ps_ap, ps_res):
        k = self.k
        for kc in range(KC):
            k.MM(ps_ap, wview[:, kc, col:col + M], src.t[:, kc, :], kc == 0, kc == KC - 1,
                 [wres, src.r[kc]], ps_res)

    def ffn(self, gT, w_up, w_down):
        k, NT = self.k, self.NT
        self.rmsnorm(gT)
        for j0 in range(0, FC, 4):
            n = min(4, FC - j0)
            gv, gr = self.wload(w_up, DC, j0 * 128, n * 128)
            uv, ur = self.wload(w_up, DC, DFF + j0 * 128, n * 128)
            for i in range(n):
                pg = self.bank()
                self.proj(gv, gr, i * 128, 128, self.h, DC, pg.t[:, 0:NT], pg.r)
                pu = self.bank()
                self.proj(uv, ur, i * 128, 128, self.h, DC, pu.t[:, 0:NT], pu.r)
                ta, tr = self.tmp()
                k.ACT(ta, pg.t[:, 0:NT], AF.Silu, pg.r, [tr])
                k.TT("dve", self.hid.t[:, j0 + i, :], pu.t[:, 0:NT], ta, ALU.mult, pu.r + [tr], [self.hid.r[j0 + i]])
        if DBG_STAGE == "ffnup":
            return
        for c in range(DC):
            dv, drr = self.wload(w_down, FC, c * 128, 128)
            if True:
                pd = self.bank()
                self.proj(dv, drr, 0, 128, self.hid, FC, pd.t[:, 0:NT], pd.r)
                k.STT(self.x.t[:, c, :], pd.t[:, 0:NT], 0.5, self.x.t[:, c, :], ALU.mult, ALU.add,
                      pd.r + [self.x.r[c]], [self.x.r[c]])

    def headnorm(self, ps, g, out_ap, out_res, stage=None):
        k, NT = self.k, self.NT
        k.ACT(self.sq.t[:, 0, :], ps.t[:, 0:NT], AF.Square, ps.r, [self.sq.r[0]])
        st = self.bank((4,))
        k.MM(st.t[:, 0:NT], self.c["ones"].t[:], self.sq.t[:, 0, :], True, True, [self.sq.r[0], self.c["ones"].r[0]], st.r)
        rs, rr = self.tmp()
        k.ACT(rs, st.t[:, 0:NT], AF.Ln, st.r + self.eps.r, [rr], bias=self.eps.t[:], scale=1.0 / HD)
        k.ACT(rs, rs, AF.Exp, [rr], [rr], scale=-0.5)
        k.STT(out_ap, ps.t[:, 0:NT], g.t[:, 0:1], rs, ALU.mult, ALU.mult, ps.r + [g.r[0], rr], out_res)
        if stage is not None:
            sap, sres = stage
            k.STT(sap, ps.t[:, 0:NT], g.t[:, 0:1], rs, ALU.mult, ALU.mult, ps.r + [g.r[0], rr], sres)

    def mixer_dense(self, L, kv, tok0, out_k, out_v, out_ki, out_va=None, sample=False):
        k, NT, NTT, c, dr = self.k, self.NT, self.NTT, self.c, self.dr
        w_in = dr["w_in"][L]
        self.rmsnorm(c["g_mix"][L])
        k.DMA("sp", c["gv"][L].t[:], dr["g_v"][L], [], c["gv"][L].r)
        kt0 = tok0 // 128
        wv0, wr0 = self.wload(w_in, DC, C_V, 512)
        wv1, wr1 = self.wload(w_in, DC, C_V + 512, 512)
        for tt in range(NTT):
            for half, (wv, wr) in enumerate(((wv0, wr0), (wv1, wr1))):
                pv = self.bank()
                for kc in range(DC):
                    k.MM(pv.t[:, :], self.h.t[:, kc, tt * 128:(tt + 1) * 128], wv[:, kc, :], kc == 0, kc == DC - 1,
                         [self.h.r[kc], wr], pv.r)
                k.ACT(self.vg.t[:, half * 512:(half + 1) * 512], pv.t[:, :], AF.Gelu_apprx_tanh, pv.r, self.vg.r)
            ss = self.sm.t[:, 0:1]
            k.P.op("act", lambda e, ss=ss: e.activation(out=self.vjunk.t[:], in_=self.vg.t[:], func=AF.Square, accum_out=ss),
                   self.vg.r, [self.vjunk.r[0], self.sm.r[0]])
            k.ACT(self.sm.t[:, 1:2], ss, AF.Ln, [self.sm.r[0]] + self.eps.r, [self.sm.r[1]], bias=self.eps.t[:], scale=1.0 / D)
            k.ACT(self.sm.t[:, 1:2], self.sm.t[:, 1:2], AF.Exp, [self.sm.r[1]], [self.sm.r[1]], scale=-0.5)
            j = 0
            k.STT(self.vn.t[:, j, :], self.vg.t[:], self.sm.t[:, 1:2], c["gv"][L].t[:], ALU.mult, ALU.mult,
                  self.vg.r + [self.sm.r[1], c["gv"][L].r[0]], [self.vn.r[j]])
            if out_va is not None:
                k.STT(self.vnf.t[:], self.vg.t[:], self.sm.t[:, 1:2], c["gv"][L].t[:], ALU.mult, ALU.mult,
                      self.vg.r + [self.sm.r[1], c["gv"][L].r[0]], self.vnf.r)
                out_va(self.vnf)
            WT = c["WTs"][L] if sample else c["WT"][L]
            for g in range(8):
                psv = self.bank()
                k.MM(psv.t[:, 0:128], self.vn.t[:, j, g * 128:(g + 1) * 128], WT.t[:, g, :], True, False,
                     [self.vn.r[j], WT.r[0]], psv.r)
                k.MM(psv.t[:, 0:128], c["ones1"].t[0:1, :], c["bhi"][L].t[0:1, g, :], False, False,
                     [c["ones1"].r[0], c["bhi"][L].r[0]], psv.r)
                k.MM(psv.t[:, 0:128], c["ones1"].t[0:1, :], c["blo"][L].t[0:1, g, :], False, True,
                     [c["ones1"].r[0], c["blo"][L].r[0]], psv.r)
                k.CP("act", self.attnT.t[:, g, tt * 128:(tt + 1) * 128], psv.t[:, 0:128], psv.r, [self.attnT.r[g]])
        if DBG_STAGE == "md1":
            return
        for hb in range(2):
            wv, wr = self.wload(w_in, DC, C_U + hb * 512, 512)
            for i in range(4):
                g = hb * 4 + i
                pu = self.bank()
                self.proj(wv, wr, i * 128, 128, self.h, DC, pu.t[:, 0:NT], pu.r)
                ta, tr = self.sq.t[:, g % 2, :], self.sq.r[g % 2]
                k.ACT(ta, pu.t[:, 0:NT], AF.Gelu_apprx_tanh, pu.r, [tr])
                if DBG_STAGE != "md2a":
                    k.TT("dve", self.aT.t[:, g, :], self.attnT.t[:, g, :], ta, ALU.mult, [self.attnT.r[g], tr], [self.aT.r[g]])
        if DBG_STAGE in ("md2", "md2a"):
            return
        for hb in range(2):
            wv, wr = self.wload(w_in, DC, C_Q + hb * 512, 512)
            for i in range(4):
                hh = hb * 4 + i
                pq = self.bank()
                self.proj(wv, wr, i * 128, 128, self.h, DC, pq.t[:, 0:NT], pq.r)
                self.headnorm(pq, c["gq"][L], self.QT.t[:, hh, :], [self.QT.r[hh]])
        if DBG_STAGE == "md3":
            return
        wv, wr = self.wload(w_in, DC, C_K, 512)
        for kh in range(NKV):
            pk = self.bank()
            self.proj(wv, wr, kh * 128, 128, self.h, DC, pk.t[:, 0:NT], pk.r)
            j = 0
            kres = [kv.KTr[kt0 + t] for t in range(NTT)]
            self.headnorm(pk, c["gk"][L], kv.KT.t[:, kh, tok0:tok0 + NT], kres,
                          stage=(self.kstage.t[:, j, :], [self.kstage.r[j]]))
            out_k(kh, self.kstage.t[:, j, :], self.kstage.r[j])
        for tt in range(NTT):
            pv = self.bank()
            for kc in range(DC):
                k.MM(pv.t[:, 0:256], self.h.t[:, kc, tt * 128:(tt + 1) * 128], wv[:, kc, 256:512], kc == 0, kc == DC - 1,
                     [self.h.r[kc], wr], pv.r)
            j = 0
            k.CP("act", self.vstage.t[:, j, :], pv.t[:, 0:256], pv.r, [self.vstage.r[j]])
            k.CP("pool", kv.V.t[:, kt0 + tt, :, :].rearrange("p a b -> p (a b)"), self.vstage.t[:, j, :], [self.vstage.r[j]], [kv.Vr[kt0 + tt]])
            out_v(tt, self.vstage.t[:, j, :], self.vstage.r[j])
        if DBG_STAGE == "md4":
            return
        wv, wr = self.wload(w_in, DC, C_QI, 512)
        for hh in range(IH):
            pq = self.bank()
            self.proj(wv, wr, hh * 64, 64, self.h, DC, pq.t[0:64, 0:NT], pq.r)
            k.ACT(self.QIT.t[:, hh, :], pq.t[0:64, 0:NT], AF.Copy, pq.r, [self.QIT.r[hh]], scale=ID ** -0.5)
            if self.QITz is not None:
                for b_ in range(4):
                    k.ACT(self.QITz.t[:, b_, hh, 32 * b_:32 * b_ + 32], pq.t[0:64, 32 * b_:32 * b_ + 32], AF.Copy, pq.r, self.QITz.r,
                          scale=ID ** -0.5)
        wv, wr = self.wload(w_in, DC, C_KI, 72)
        pk = self.bank()
        self.proj(wv, wr, 0, 64, self.h, DC, pk.t[0:64, 0:NT], pk.r)
        k.CP("act", self.kstage.t[0:64, 0, :], pk.t[0:64, 0:NT], pk.r, [self.kstage.r[0]])
        k.CP("pool", kv.KIT.t[:, tok0:tok0 + NT], self.kstage.t[0:64, 0, :], [self.kstage.r[0]], [kv.KIr[kt0 + t] for t in range(NTT)])
        out_ki(self.kstage.t[0:64, 0, :], self.kstage.r[0])
        for tt in range(NTT):
            pw = self.bank()
            for kc in range(DC):
                k.MM(pw.t[:, 0:8], self.h.t[:, kc, tt * 128:(tt + 1) * 128], wv[:, kc, 64:72], kc == 0, kc == DC - 1,
                     [self.h.r[kc], wr], pw.r)
            k.ACT(self.wtok.t[:, tt, :], pw.t[:, 0:8], AF.Copy, pw.r, [self.wtok.r[tt]], scale=IH ** -0.5)

    def index_scores(self, kv, tt, nk, scores, prange=(0, 128), col0=0):
        k = self.k
        p0, p1 = prange
        for ch in range((nk + 511) // 512):
            w = min(512, nk - ch * 512)
            kts = [kv.KIr[t] for t in range(ch * 4, ch * 4 + (w + 127) // 128)]
            for hh in range(IH):
                pd = self.bank((0, 1))
                k.MM(pd.t[:, 0:w], self.QIT.t[:, hh, tt * 128:(tt + 1) * 128], kv.KIT.t[:, ch * 512:ch * 512 + w], True, True,
                     [self.QIT.r[hh]] + kts, pd.r)
                ti = self.tmp_i % 3
                self.tmp_i += 1
                ta, tr = self.tmpf.t[:, ti, :], self.tmpf.r[ti]
                k.ACT(ta[p0:p1, 0:w], pd.t[p0:p1, 0:w], AF.Relu, pd.r, [tr])
                sc = scores.t[p0:p1, col0 + ch * 512:col0 + ch * 512 + w]
                if hh == 0:
                    k.TS("dve", sc, ta[p0:p1, 0:w], self.wtok.t[p0:p1, tt, 0:1], None, ALU.mult, None,
                         [tr, self.wtok.r[tt]], scores.r)
                else:
                    k.STT(sc, ta[p0:p1, 0:w], self.wtok.t[p0:p1, tt, hh:hh + 1], sc, ALU.mult, ALU.add,
                          [tr, self.wtok.r[tt]] + scores.r, scores.r)

    def index_scores_pe(self, kv, tt, nk, scores, diag):
        k = self.k
        for hh in range(IH):
            k.TS("dve", diag.t[:, hh, :], self.c["I4"].t[:, 0, :], self.wtok.t[:, tt, hh:hh + 1], None, ALU.mult, None,
                 [self.c["I4"].r[0], self.wtok.r[tt]], diag.r)
        for ch in range((nk + 511) // 512):
            w = min(512, nk - ch * 512)
            kts = [kv.KIr[t] for t in range(ch * 4, ch * 4 + (w + 127) // 128)]
            psc = self.banks[1]
            for hh in range(IH):
                pd = self.banks[0]
                k.MM(pd.t[:, 0:w], self.QIT.t[:, hh, tt * 128:(tt + 1) * 128], kv.KIT.t[:, ch * 512:ch * 512 + w], True, True,
                     [self.QIT.r[hh]] + kts, pd.r)
                j = hh % 2
                k.ACT(self.sq.t[:, j, 0:w], pd.t[:, 0:w], AF.Relu, pd.r, [self.sq.r[j]])
                k.MM(psc.t[:, 0:w], diag.t[:, hh, :], self.sq.t[:, j, 0:w], hh == 0, hh == IH - 1, diag.r + [self.sq.r[j]], psc.r)
            k.CP("act", scores.t[:, ch * 512:ch * 512 + w], psc.t[:, 0:w], psc.r, scores.r)

    def bisect_mask(self, scores, ncol, ksel, maskb, junk, bis):
        k = self.k
        lo, mid, g = bis.t[:, 0:1], bis.t[:, 1:2], bis.t[:, 3:4]
        cnt = bis.t[:, 2:3]
        cnt8 = bis.t[:, 8:32]
        JW = junk.t.shape[1]
        nchunk = (ncol + JW - 1) // JW
        W = BIS_W
        k.MS("dve", mid, BIS_LO + 0.5 * W, bis.r)
        for it in range(BIS_ITERS):
            for ci in range(nchunk):
                w = min(JW, ncol - ci * JW)
                k.TS("dve", junk.t[:, 0:w], scores.t[:, ci * JW:ci * JW + w], mid, 0.0, ALU.is_ge, ALU.add,
                     scores.r + bis.r, junk.r + bis.r, accum_out=cnt8[:, ci:ci + 1])
            if nchunk > 1:
                k.P.op("dve", lambda e: e.reduce_sum(out=cnt, in_=cnt8[:, 0:nchunk], axis=mybir.AxisListType.X), bis.r, bis.r)
                cc = cnt
            else:
                cc = cnt8[:, 0:1]
            if it < BIS_ITERS - 1:
                k.TS("dve", g, cc, ksel - 0.5, 0.5 * W, ALU.is_ge, ALU.mult, bis.r, bis.r)
                k.STT(mid, g, -0.25 * W, mid, ALU.add, ALU.add, bis.r, bis.r)
            else:
                k.TS("dve", g, cc, ksel - 0.5, 0.5 * W, ALU.is_ge, ALU.mult, bis.r, bis.r)
                k.STT(lo, g, -0.5 * W, mid, ALU.add, ALU.add, bis.r, bis.r)
            W *= 0.5
        if maskb is not None:
            k.TS("dve", maskb.t[:, 0:ncol], scores.t[:, 0:ncol], lo, MB, ALU.is_lt, ALU.mult, scores.r + bis.r, maskb.r)

    def attend(self, kv, tt, ntile, maskb, mcol0, nslot, slot0, first, last, acc):
        k = self.k
        OT, RS = acc
        N = 4 * nslot
        for kh in range(NKV):
            for kt in range(ntile):
                pS = self.bank((2, 3))
                outv = pS.t[:, 0:N].rearrange("p (g t) -> p g t", g=4)
                q0 = tt * 128 + slot0
                k.MM(outv, kv.KT.t[:, kh, kt * 128:(kt + 1) * 128], self.QT.t[:, kh * 4:(kh + 1) * 4, q0:q0 + nslot], True, False,
                     [kv.KTr[kt]] + self.QT.r[kh * 4:(kh + 1) * 4], pS.r)
                k.MM(outv, maskb.t[:, mcol0 + kt * 128:mcol0 + (kt + 1) * 128], self.c["I4"].t[:, :, slot0:slot0 + nslot], False, True,
                     maskb.r + [self.c["I4"].r[0]], pS.r)
                i = self.tmp_i % 3
                self.tmp_i += 1
                pT = self.c["pT"].t[:, i, 0:N]
                pr = self.c["pT"].r[i]
                k.ACT(pT, pS.t[:, 0:N], AF.Exp, pS.r, [pr], scale=HD ** -0.5)
                st = first and kt == 0
                sp = last and kt == ntile - 1
                k.MM(OT[kh].t[:, 0:N], kv.V.t[:, kt, kh, :], pT, st, sp, [kv.Vr[kt], pr], OT[kh].r)
                k.MM(RS[kh].t[:, 0:N], self.c["ones"].t[:], pT, st, sp, [self.c["ones"].r[0], pr], RS[kh].r)

    def attn_finish(self, tt, nslot, slot0, acc):
        k = self.k
        OT, RS = acc
        N = 4 * nslot
        for kh in range(NKV):
            ta, tr = self.tmp()
            k.ACT(ta[:, 0:N], RS[kh].t[:, 0:N], AF.Ln, RS[kh].r, [tr])
            k.ACT(ta[:, 0:N], ta[:, 0:N], AF.Exp, [tr], [tr], scale=-1.0)
            q0 = tt * 128 + slot0
            k.TT("dve", self.attnT.t[:, kh * 4:(kh + 1) * 4, q0:q0 + nslot], OT[kh].t[:, 0:N].rearrange("p (g t) -> p g t", g=4),
                 ta[:, 0:N].rearrange("p (g t) -> p g t", g=4), ALU.mult, OT[kh].r + [tr], self.attnT.r[kh * 4:(kh + 1) * 4])

    def merge_out(self, L):
        k, NT, dr = self.k, self.NT, self.dr
        w_in = dr["w_in"][L]
        for hb in range(2):
            wa, ra = self.wload(dr["w_pa"][L], DC, hb * 512, 512)
            wg, rg = self.wload(w_in, DC, C_GA + hb * 512, 512)
            for i in range(4):
                c = hb * 4 + i
                pa = self.bank()
                self.proj(wa, ra, i * 128, 128, self.aT, DC, pa.t[:, 0:NT], pa.r)
                pg = self.bank()
                self.proj(wg, rg, i * 128, 128, self.h, DC, pg.t[:, 0:NT], pg.r)
                ta, tr = self.tmp()
                k.ACT(ta, pg.t[:, 0:NT], AF.Sigmoid, pg.r, [tr])
                k.TT("dve", self.hid.t[:, c, :], pa.t[:, 0:NT], ta, ALU.mult, pa.r + [tr], [self.hid.r[c]])
        for hb in range(2):
            wb, rb = self.wload(dr["w_pb"][L], DC, hb * 512, 512)
            wg, rg = self.wload(w_in, DC, C_GB + hb * 512, 512)
            for i in range(4):
                c = hb * 4 + i
                pb = self.bank()
                self.proj(wb, rb, i * 128, 128, self.attnT, DC, pb.t[:, 0:NT], pb.r)
                pg = self.bank()
                self.proj(wg, rg, i * 128, 128, self.h, DC, pg.t[:, 0:NT], pg.r)
                ta, tr = self.tmp()
                k.ACT(ta, pg.t[:, 0:NT], AF.Sigmoid, pg.r, [tr])
                tb, trb = self.sq.t[:, c % 2, :], self.sq.r[c % 2]
                k.TT("dve", tb, pb.t[:, 0:NT], ta, ALU.mult, pb.r + [tr], [trb])
                k.TT("dve", self.mT.t[:, c, :], tb, self.hid.t[:, c, :], ALU.add, [trb, self.hid.r[c]], [self.mT.r[c]])
        for hb in range(2):
            wo, ro = self.wload(dr["w_out"][L], DC, hb * 512, 512)
            for i in range(4):
                c = hb * 4 + i
                po = self.bank()
                self.proj(wo, ro, i * 128, 128, self.mT, DC, po.t[:, 0:NT], po.r)
                k.TT("dve", self.x.t[:, c, :], po.t[:, 0:NT], self.x.t[:, c, :], ALU.add, po.r + [self.x.r[c]], [self.x.r[c]])


class KVCache:
    def __init__(self, k, name, nkeys, with_kit=True):
        nt = nkeys // 128
        self.KT = k.sb(name + "KT", [128, NKV, nkeys], BF16)
        self.V = k.sb(name + "V", [128, nt, NKV, HD], BF16)
        self.KIT = k.sb(name + "KIT", [64, nkeys], BF16) if with_kit else None
        self.KTr = [Res() for _ in range(nt)]
        self.Vr = [Res() for _ in range(nt)]
        self.KIr = [Res() for _ in range(nt)]


def load_consts(k, dr, cfg, sample):
    c = {}
    Lr = range(cfg.depth)
    c["ones"] = k.sb("ones", [128, 128], BF16)
    k.MS("pool", c["ones"].t[:], 1.0, c["ones"].r)
    c["ones1"] = k.sb("ones1", [1, 128], BF16)
    k.MS("pool", c["ones1"].t[:], 1.0, c["ones1"].r)
    c["I4"] = k.sb("I4", [128, 4, 128], BF16)
    k.DMA("pool", c["I4"].t[:], dr["I4"], [], c["I4"].r)
    c["tri"] = k.sb("tri", [128, 128], F32)
    k.DMA("sp", c["tri"].t[:], dr["tri"], [], c["tri"].r)
    triu = k.sb("triu", [128, 128], BF16)
    k.DMA("pool", triu.t[:], dr["triu"], [], triu.r)
    c["pT"] = k.sb("pT", [128, 3, 512], BF16, 3)
    for nm in ("g_ffn1", "g_mix", "g_ffn2"):
        c[nm] = []
        for L in Lr:
            t = k.sb("%s%d" % (nm, L), [128, DC], F32)
            k.DMA("sp", t.t[:], dr[nm][L], [], t.r)
            c[nm].append(t)
    for nm, src in (("gq", "g_q"), ("gk", "g_k")):
        c[nm] = []
        for L in Lr:
            t = k.sb("%s%d" % (nm, L), [128, 1], F32)
            k.DMA("sp", t.t[:], dr[src][L], [], t.r)
            c[nm].append(t)
    gvt = k.sb("gvt", [128, D], F32)
    c["gv"] = [gvt for L in Lr]
    c["bhi"], c["blo"], c["WT"], c["WTs"] = [], [], [], []
    bf = k.sb("bf", [1, 8, 128], F32)
    bh32 = k.sb("bh32", [1, 8, 128], F32)
    for L in Lr:
        if sample:
            for b in range(4):
                k.DMA("sp", bf.t[0:1, :, 32 * b:32 * b + 32], dr["b_s"][L][:, :, 0:32], [], bf.r)
        else:
            k.DMA("sp", bf.t[:], dr["b_s"][L], [], bf.r)
        bhi = k.sb("bhi%d" % L, [1, 8, 128], BF16)
        blo = k.sb("blo%d" % L, [1, 8, 128], BF16)
        k.CP("dve", bhi.t[:], bf.t[:], bf.r, bhi.r)
        k.CP("dve", bh32.t[:], bhi.t[:], bhi.r, bh32.r)
        k.TT("dve", blo.t[:], bf.t[:], bh32.t[:], ALU.subtract, bf.r + bh32.r, blo.r)
        c["bhi"].append(bhi)
        c["blo"].append(blo)
        WT = k.sb("WT%d" % L, [128, 8, 128], BF16)
        if sample:
            k.MS("pool", WT.t[:], 0.0, WT.r)
            for b in range(4):
                k.DMA("pool", WT.t[32 * b:32 * b + 4, :, 32 * b:32 * b + 4],
                      dr["wsT"][L][:, 0:4, 0:4].rearrange("g s t -> s g t"), [], WT.r)
        else:
            k.DMA("pool", WT.t[:], dr["wsT"][L].rearrange("g s t -> s g t"), [], WT.r)
        for g in range(8):
            k.TT("dve", WT.t[:, g, :], triu.t[:], WT.t[:, g, :], ALU.mult, WT.r + triu.r, WT.r)
        c["WTs" if sample else "WT"].append(WT)
    return c


def build_prompt(nc, dr, cfg):
    NT = 512
    with ExitStack() as stack:
        P = Prog(nc)
        k = K(nc, P, stack, "p_")
        c = load_consts(k, dr, cfg, False)
        dn = Dense(k, NT, dr, c, wmode=("bf16" if SAVE_BF16 else "cast"), out_q=("pool" if SAVE_BF16 else "sp"))
        OQ = dn.out_q
        S = cfg.S
        caches = [KVCache(k, "c%d" % L, S) for L in range(cfg.depth)]
        scores = [k.sb("scores%d" % i, [128, S], F32) for i in range(1)]
        sc2 = Tl(dn.hidraw.t[:, 2048:2048 + S], 0)
        sc2.r = dn.hidraw.r[8:16]
        scores.append(sc2)
        maskb = [k.sb("maskb%d" % i, [128, S], BF16) for i in range(1)]
        junk = Tl(dn.hidraw.t[:, 4096:4096 + S // 2].bitcast(BF16), 0)
        junk.r = dn.hidraw.r[16:20]
        diag = k.sb("diag", [128, IH, 128], BF16)
        bis = k.sb("bis", [128, 32], F32)
        OT = [dn.banks[4], dn.banks[5]]
        RS = [dn.banks[6], dn.banks[7]]
        qn = 0
        for s in range(cfg.nseq):
            for blk in range(S // NT):
                t0 = blk * NT
                for c_ in range(DC):
                    k.DMA(OQ, dn.x.t[:, c_, :], dr["xT"][s, c_ * 128:(c_ + 1) * 128, t0:t0 + NT], [], [dn.x.r[c_]])
                for L in range(cfg.depth):
                    kv = caches[L]
                    if DBG_STAGE == "none":
                        continue
                    if DBG_STAGE == "norm":
                        dn.rmsnorm(c["g_ffn1"][L])
                        continue
                    dn.ffn(c["g_ffn1"][L], dr["w_up1"][L], dr["w_down1"][L])
                    if DBG_STAGE in ("ffn", "ffnup"):
                        continue

                    def out_k(kh, ap, res, L=L, s=s, t0=t0):
                        k.DMA(OQ, dr["o_kT"][L, s, kh * 128:(kh + 1) * 128, t0:t0 + NT], ap, [res], [], is_out=True)

                    def out_v(tt, ap, res, L=L, s=s, t0=t0):
                        k.DMA(OQ, dr["o_v"][L, s, t0 + tt * 128:t0 + (tt + 1) * 128, :], ap, [res], [], is_out=True)

                    def out_ki(ap, res, L=L, s=s, t0=t0):
                        k.DMA(OQ, dr["o_kiT"][L, s, :, t0:t0 + NT], ap, [res], [], is_out=True)

                    dn.mixer_dense(L, kv, t0, out_k, out_v, out_ki)
                    if DBG_STAGE in ("dense", "md1", "md2", "md2a", "md3", "md4"):
                        continue
                    def idx(tt, kv=kv, blk=blk):
                        qg = blk * 4 + tt
                        nk = (qg + 1) * 128
                        sc = scores[tt % 2]
                        dn.index_scores_pe(kv, tt, nk, sc, diag)
                        k.TT("dve", sc.t[:, nk - 128:nk], c["tri"].t[:], sc.t[:, nk - 128:nk], ALU.add, sc.r + c["tri"].r, sc.r)

                    def bisect(tt, blk=blk):
                        qg = blk * 4 + tt
                        nk = (qg + 1) * 128
                        if nk > cfg.ksel:
                            dn.bisect_mask(scores[tt % 2], nk, cfg.ksel, None, junk, bis)

                    def mask(tt, blk=blk):
                        qg = blk * 4 + tt
                        nk = (qg + 1) * 128
                        sc, mb = scores[tt % 2], maskb[0]
                        if nk > cfg.ksel:
                            k.TS("dve", mb.t[:, 0:nk], sc.t[:, 0:nk], bis.t[:, 0:1], MB, ALU.is_lt, ALU.mult, sc.r + bis.r, mb.r)
                        else:
                            k.TS("dve", mb.t[:, 0:nk], sc.t[:, 0:nk], -1.0e29, MB, ALU.is_lt, ALU.mult, sc.r, mb.r)

                    def att(tt, kv=kv, blk=blk):
                        qg = blk * 4 + tt
                        dn.attend(kv, tt, qg + 1, maskb[0], 0, 128, 0, True, True, (OT, RS))
                        dn.attn_finish(tt, 128, 0, (OT, RS))

                    nq = NT // 128
                    idx(0)
                    bisect(0)
                    mask(0)
                    for tt in range(nq):
                        if tt + 1 < nq:
                            idx(tt + 1)
                            bisect(tt + 1)
                        att(tt)
                        if tt + 1 < nq:
                            mask(tt + 1)
                    dn.merge_out(L)
                    dn.ffn(c["g_ffn2"][L], dr["w_up2"][L], dr["w_down2"][L])
                for c_ in range(DC):
                    k.DMA(OQ, dr["o_yT"][s, c_ * 128:(c_ + 1) * 128, t0:t0 + NT], dn.x.t[:, c_, :], [dn.x.r[c_]], [], is_out=True)
        P.emit("p")


def build_sample(nc, dr, cfg):
    NT = 128
    npg = cfg.npg
    RS_ROWS = 4
    nslab = 128 // RS_ROWS
    npast = npg * 128
    with ExitStack() as stack:
        P = Prog(nc)
        k = K(nc, P, stack, "s_")
        c = load_consts(k, dr, cfg, True)
        dn = Dense(k, NT, dr, c, wmode=("save" if SAVE_BF16 else "cast"), nslots=2)
        ident = k.sb("ident", [128, 128], F32)
        k.DMA("sp", ident.t[:], dr["ident"], [], ident.r)
        mnew = k.sb("mnew", [128, 128], F32)
        k.DMA("sp", mnew.t[:], dr["mnew"], [], mnew.r)
        own = KVCache(k, "own", 128)
        slab = [KVCache(k, "slab%d" % i, RS_ROWS * 128, with_kit=False) for i in range(2)]
        ncol = npast + 128
        scores = k.sb("scores", [128, ncol], F32)
        mslab = k.sb("mslab", [128, 2, RS_ROWS * 128], BF16, 2)
        junk = k.sb("junk", [128, 1024], BF16)
        bis = k.sb("bis", [128, 32], F32)
        idx = k.sb("idx", [128, 4], I32)
        for b in range(cfg.ndec):
            k.DMA("sp", idx.t[0:npg, b:b + 1], dr["page_table"][b, :].rearrange("(n o) -> n o", o=1), [], idx.r)
        idxs = k.sb("idxs", [128, cfg.ndec, nslab], I32)
        for sl in range(nslab):
            k.TS("dve", idxs.t[:, :, sl], idx.t[:, 0:cfg.ndec], float(nslab), float(sl), ALU.mult, ALU.add, idx.r, idxs.r)
        kig8 = k.sb("kig8", [128, 4, RS_ROWS * ID], F32, 4)
        kit8 = k.sb("kit8", [64, 8, RS_ROWS * 128], BF16, 8)
        dn.QITz = k.sb("QITz", [64, 4, IH, 128], BF16)
        k.MS("pool", dn.QITz.t[:], 0.0, dn.QITz.r)
        gkb = [k.sb("gkb%d" % i, [128, RS_ROWS, NKV * HD], BF16) for i in range(2)]
        kg = [k.sb("kg%d" % i, [128, RS_ROWS, NKV * HD], F32) for i in range(2)]
        vg = [k.sb("vg%d" % i, [128, RS_ROWS, NKV * HD], F32) for i in range(2)]
        OT = [dn.banks[4], dn.banks[5]]
        RS = [dn.banks[6], dn.banks[7]]
        for c_ in range(DC):
            k.DMA("sp", dn.x.t[:, c_, :], dr["xsT"][c_ * 128:(c_ + 1) * 128, :], [], [dn.x.r[c_]])
        gi = 0
        for L in range(cfg.depth):
            dn.ffn(c["g_ffn1"][L], dr["w_up1"][L], dr["w_down1"][L])

            def out_k(kh, ap, res, L=L):
                k.DMA("sp", dr["o_ksT"][L, kh * 128:(kh + 1) * 128, :], ap, [res], [], is_out=True)

            def out_v(tt, ap, res, L=L):
                k.DMA("sp", dr["o_vs"][L], ap, [res], [], is_out=True)

            def out_ki(ap, res, L=L):
                k.DMA("sp", dr["o_kisT"][L], ap, [res], [], is_out=True)

            def out_va(t, L=L):
                k.DMA("sp", dr["o_vas"][L], t.t[:], t.r, [], is_out=True)

            dn.mixer_dense(L, own, 0, out_k, out_v, out_ki, out_va=out_va, sample=True)
            assert RS_ROWS == 4 and npg == 128
            W_ = RS_ROWS * 128
            for sl in range(nslab):
                slots = []
                for b in range(cfg.ndec):
                    i8 = gi % 8
                    i4 = gi % 4
                    gi += 1
                    slots.append(i8)
                    src = dr["cache_kidx"][L].rearrange("p (s r) d -> (p s) (r d)", r=RS_ROWS)
                    k.P.dma("pool", lambda e, i4=i4, src=src, b=b, sl=sl: e.indirect_dma_start(
                        out=kig8.t[:, i4, :], out_offset=None, in_=src,
                        in_offset=bass.IndirectOffsetOnAxis(ap=idxs.t[:, b, sl:sl + 1], axis=0)), idxs.r, [kig8.r[i4]])
                    pt = dn.bank((0, 1))
                    for r in range(RS_ROWS):
                        k.TR(pt.t[0:64, r * 128:(r + 1) * 128], kig8.t[:, i4, r * ID:(r + 1) * ID], ident.t[:], [kig8.r[i4]] + ident.r, pt.r)
                    k.CP("act", kit8.t[:, i8, :], pt.t[0:64, 0:W_], pt.r, [kit8.r[i8]])
                for hh in range(IH):
                    pd = dn.bank((2, 3))
                    for b in range(cfg.ndec):
                        k.MM(pd.t[:, 0:W_], dn.QITz.t[:, b, hh, :], kit8.t[:, slots[b], :], b == 0, b == cfg.ndec - 1,
                             dn.QITz.r + [kit8.r[slots[b]]], pd.r)
                    ta, tr = dn.tmp()
                    ti = (dn.tmp_i - 1) % 3
                    ta = dn.tmpf.t[:, ti, :]
                    k.ACT(ta[:, 0:W_], pd.t[:, 0:W_], AF.Relu, pd.r, [tr])
                    sc = scores.t[:, sl * W_:(sl + 1) * W_]
                    if hh == 0:
                        k.TS("dve", sc, ta[:, 0:W_], dn.wtok.t[:, 0, 0:1], None, ALU.mult, None, [tr, dn.wtok.r[0]], scores.r)
                    else:
                        k.STT(sc, ta[:, 0:W_], dn.wtok.t[:, 0, hh:hh + 1], sc, ALU.mult, ALU.add, [tr, dn.wtok.r[0]] + scores.r, scores.r)
            dn.index_scores(own, 0, 128, scores, prange=(0, 128), col0=npast)
            k.TT("dve", scores.t[:, npast:ncol], mnew.t[:], scores.t[:, npast:ncol], ALU.add, scores.r + mnew.r, scores.r)
            dn.bisect_mask(scores, ncol, cfg.ksel_s, None, junk, bis)
            if DBG_STAGE == "sdbg" and L == 0:
                k.DMA("sp", dr["o_dbg_sc"], scores.t[:], scores.r, [], is_out=True)
                k.DMA("sp", dr["o_dbg_bis"], bis.t[:, 0:1], bis.r, [], is_out=True)
            for b in range(cfg.ndec):
                for sl in range(nslab):
                    r0 = sl * RS_ROWS
                    gk, gv = kg[gi % 2], vg[gi % 2]
                    sb_ = slab[gi % 2]
                    gi += 1
                    srck = dr["cache_k"][L].rearrange("p (s r) d -> (p s) (r d)", r=RS_ROWS)
                    srcv = dr["cache_v"][L].rearrange("p (s r) d -> (p s) (r d)", r=RS_ROWS)
                    k.P.dma("pool", lambda e, gk=gk, srck=srck, b=b, sl=sl: e.indirect_dma_start(
                        out=gk.t[:, :, :].rearrange("p r d -> p (r d)"), out_offset=None, in_=srck,
                        in_offset=bass.IndirectOffsetOnAxis(ap=idxs.t[:, b, sl:sl + 1], axis=0)), idxs.r, gk.r)
                    k.P.dma("pool", lambda e, gv=gv, srcv=srcv, b=b, sl=sl: e.indirect_dma_start(
                        out=gv.t[:, :, :].rearrange("p r d -> p (r d)"), out_offset=None, in_=srcv,
                        in_offset=bass.IndirectOffsetOnAxis(ap=idxs.t[:, b, sl:sl + 1], axis=0)), idxs.r, gv.r)
                    for r in range(RS_ROWS):
                        k.CP("dve", sb_.V.t[:, r, :, :].rearrange("p a b -> p (a b)"), gv.t[:, r, :], gv.r, [sb_.Vr[r]])
                    gb_ = gkb[gi % 2]
                    k.CP("dve", gb_.t[:].rearrange("p r d -> p (r d)"), gk.t[:].rearrange("p r d -> p (r d)"), gk.r, gb_.r)
                    pt = dn.bank((0, 1))
                    ptb = pt.t[:].bitcast(BF16)
                    for kh in range(NKV):
                        for r in range(RS_ROWS):
                            c0 = (kh * RS_ROWS + r) * 128
                            k.TR(ptb[:, c0:c0 + 128], gb_.t[:, r, kh * 128:(kh + 1) * 128], c["I4"].t[:, 0, :], gb_.r + c["I4"].r, pt.r)
                    for kh in range(NKV):
                        k.CP("act", sb_.KT.t[:, kh, 0:RS_ROWS * 128], ptb[:, kh * RS_ROWS * 128:(kh + 1) * RS_ROWS * 128], pt.r,
                             [sb_.KTr[r] for r in range(RS_ROWS)])
                    mi = gi % 2
                    mv = Tl(mslab.t[:, mi, :])
                    mv.r = [mslab.r[mi]]
                    W_ = RS_ROWS * 128
                    k.TS("dve", mv.t[:, 0:W_], scores.t[:, sl * W_:(sl + 1) * W_], bis.t[:, 0:1], MB, ALU.is_lt, ALU.mult, scores.r + bis.r, mv.r)
                    dn.attend(sb_, 0, RS_ROWS, mv, 0, 32, 32 * b, sl == 0, False, (OT, RS))
                mv = Tl(mslab.t[:, 0, :])
                mv.r = [mslab.r[0]]
                k.TS("dve", mv.t[:, 0:128], scores.t[:, npast:ncol], bis.t[:, 0:1], MB, ALU.is_lt, ALU.mult, scores.r + bis.r, mv.r)
                dn.attend(own, 0, 1, mv, 0, 32, 32 * b, False, True, (OT, RS))
                dn.attn_finish(0, 32, 32 * b, (OT, RS))
            dn.merge_out(L)
            dn.ffn(c["g_ffn2"][L], dr["w_up2"][L], dr["w_down2"][L])
        for c_ in range(DC):
            k.DMA("sp", dr["o_ysT"][c_ * 128:(c_ + 1) * 128, :], dn.x.t[:, c_, :], [dn.x.r[c_]], [], is_out=True)
        P.emit("s")


def build_nc(cfg, do_prompt=True, do_sample=True):
    nc = bass.Bass("TRN2", target_bir_lowering=False)
    Ld = cfg.depth
    S, ns = cfg.S, cfg.nseq

    def din(name, shape, dt=F32):
        return nc.dram_tensor(name, list(shape), dt, kind="ExternalInput").ap()

    def dout(name, shape):
        return nc.dram_tensor(name, list(shape), F32, kind="ExternalOutput").ap()

    dr = {}
    dr["xT"] = din("xT", [ns, D, S])
    dr["xsT"] = din("xsT", [D, 128])
    dr["cache_k"] = [din("cache_k%d" % L, [cfg.pool, 128, NKV * HD]) for L in range(Ld)]
    dr["cache_v"] = [din("cache_v%d" % L, [cfg.pool, 128, NKV * HD]) for L in range(Ld)]
    dr["cache_kidx"] = [din("cache_kidx%d" % L, [cfg.pool, 128, ID]) for L in range(Ld)]
    dr["page_table"] = din("page_table", [cfg.ndec, cfg.npg], I32)
    for nm in ("g_ffn1", "g_mix", "g_ffn2"):
        dr[nm] = din(nm, [Ld, 128, DC])
    dr["g_q"] = din("g_q", [Ld, 128, 1])
    dr["g_k"] = din("g_k", [Ld, 128, 1])
    dr["g_v"] = din("g_v", [Ld, 128, D])
    dr["b_s"] = din("b_s", [Ld, 1, 8, 128])
    dr["wsT"] = din("wsT", [Ld, 8, 128, 128])
    for nm, kk, nn in (("w_up1", D, 2 * DFF), ("w_down1", DFF, D), ("w_in", D, DIN), ("w_pa", D, D), ("w_pb", D, D),
                       ("w_out", D, D), ("w_up2", D, 2 * DFF), ("w_down2", DFF, D)):
        dr[nm] = din(nm, [Ld, kk, nn])
    for nm, kk, nn in (("w_up1", D, 2 * DFF), ("w_down1", DFF, D), ("w_in", D, DIN), ("w_pa", D, D), ("w_pb", D, D),
                       ("w_out", D, D), ("w_up2", D, 2 * DFF), ("w_down2", DFF, D)):
        dr[nm + "_bf"] = nc.dram_tensor(nm + "_bf", [Ld, kk, nn], BF16, kind="Internal").ap()
    dr["I4"] = din("I4", [128, 4, 128])
    dr["tri"] = din("tri", [128, 128])
    dr["triu"] = din("triu", [128, 128])
    dr["ident"] = din("ident", [128, 128])
    dr["mnew"] = din("mnew", [128, 128])
    dr["o_yT"] = dout("o_yT", [ns, D, S])
    dr["o_kT"] = dout("o_kT", [Ld, ns, NKV * HD, S])
    dr["o_v"] = dout("o_v", [Ld, ns, S, NKV * HD])
    dr["o_kiT"] = dout("o_kiT", [Ld, ns, ID, S])
    dr["o_ysT"] = dout("o_ysT", [D, 128])
    dr["o_ksT"] = dout("o_ksT", [Ld, NKV * HD, 128])
    dr["o_vs"] = dout("o_vs", [Ld, 128, NKV * HD])
    dr["o_kisT"] = dout("o_kisT", [Ld, ID, 128])
    dr["o_vas"] = dout("o_vas", [Ld, 128, D])
    if DBG_STAGE == "sdbg":
        dr["o_dbg_sc"] = dout("o_dbg_sc", [128, cfg.npg * 128 + 128])
        dr["o_dbg_bis"] = dout("o_dbg_bis", [128, 1])
    if do_sample:
        build_sample(nc, dr, cfg)
    if do_prompt:
        build_prompt(nc, dr, cfg)
    return nc


def host_consts():
    I4 = np.tile(np.eye(128, dtype=np.float32)[:, None, :], (1, 4, 1))
    t = np.arange(128)
    tri = np.where(t[None, :] <= t[:, None], 0.0, NEG).astype(np.float32)
    triu = (t[None, :] >= t[:, None]).astype(np.float32)
    ident = np.eye(128, dtype=np.float32)
    b_ = t // 32
    tl = t % 32
    ok = (b_[:, None] == b_[None, :]) & (tl[None, :] <= tl[:, None]) & (tl[None, :] < 4)
    mnew = np.where(ok, 0.0, NEG).astype(np.float32)
    return {"I4": I4, "tri": tri, "triu": triu, "ident": ident, "mnew": mnew}


def make_in_maps(inp, cfg, ncores):
    Ld = cfg.depth
    f = np.float32
    shared = host_consts()
    for nm in ("g_ffn1", "g_mix", "g_ffn2"):
        shared[nm] = np.ascontiguousarray(np.asarray(inp[nm], f).reshape(Ld, DC, 128).transpose(0, 2, 1))
    shared["g_q"] = np.asarray(inp["g_q"], f).reshape(Ld, 128, 1)
    shared["g_k"] = np.asarray(inp["g_k"], f).reshape(Ld, 128, 1)
    shared["g_v"] = np.ascontiguousarray(np.broadcast_to(np.asarray(inp["g_v"], f)[:, None, :], (Ld, 128, D)))
    shared["b_s"] = np.asarray(inp["b_s"], f).reshape(Ld, 1, 8, 128)
    shared["wsT"] = np.ascontiguousarray(np.asarray(inp["w_s"], f).transpose(0, 1, 3, 2))
    for nm in ("w_up1", "w_down1", "w_in", "w_pa", "w_pb", "w_out", "w_up2", "w_down2"):
        shared[nm] = np.asarray(inp[nm], f)
    pool = cfg.pool
    ck = np.asarray(inp["cache_k"], f).reshape(Ld, pool, 128, NKV * HD)
    cv = np.asarray(inp["cache_v"], f).reshape(Ld, pool, 128, NKV * HD)
    cki = np.asarray(inp["cache_kidx"], f)
    for L in range(Ld):
        shared["cache_k%d" % L] = ck[L]
        shared["cache_v%d" % L] = cv[L]
        shared["cache_kidx%d" % L] = cki[L]
    xp = np.asarray(inp["x_prompt"], f)
    xs = np.asarray(inp["x_sample"], f)
    pt = np.asarray(inp["page_table"], np.int32)
    maps = []
    for ci in range(ncores):
        m = dict(shared)
        m["xT"] = np.ascontiguousarray(xp[ci * cfg.nseq:(ci + 1) * cfg.nseq].transpose(0, 2, 1))
        xsT = np.zeros((D, 128), f)
        for b in range(cfg.ndec):
            xsT[:, 32 * b:32 * b + cfg.tdec] = xs[ci * cfg.ndec + b].T
        m["xsT"] = xsT
        m["page_table"] = np.ascontiguousarray(pt[ci * cfg.ndec:(ci + 1) * cfg.ndec])
        maps.append(m)
    return maps


def assemble(results, cfg, ncores):
    Ld = cfg.depth
    ns, S, nd, T = cfg.nseq, cfg.S, cfg.ndec, cfg.tdec
    B, DB = ns * ncores, nd * ncores
    y = np.empty((B, S, D), np.float32)
    nk = np.empty((Ld, B, S, NKV, HD), np.float32)
    nv = np.empty((Ld, B, S, NKV, HD), np.float32)
    nki = np.empty((Ld, B, S, ID), np.float32)
    ys = np.empty((DB, T, D), np.float32)
    ks = np.empty((Ld, DB, T, NKV, HD), np.float32)
    vs = np.empty((Ld, DB, T, NKV, HD), np.float32)
    kis = np.empty((Ld, DB, T, ID), np.float32)
    vas = np.empty((Ld, DB, T, D), np.float32)
    for ci, r in enumerate(results):
        sl = slice(ci * ns, (ci + 1) * ns)
        y[sl] = r["o_yT"].transpose(0, 2, 1)
        nk[:, sl] = r["o_kT"].transpose(0, 1, 3, 2).reshape(Ld, ns, S, NKV, HD)
        nv[:, sl] = r["o_v"].reshape(Ld, ns, S, NKV, HD)
        nki[:, sl] = r["o_kiT"].transpose(0, 1, 3, 2)
        for b in range(nd):
            gb = ci * nd + b
            cols = slice(32 * b, 32 * b + T)
            ys[gb] = r["o_ysT"][:, cols].T
            ks[:, gb] = r["o_ksT"][:, :, cols].transpose(0, 2, 1).reshape(Ld, T, NKV, HD)
            vs[:, gb] = r["o_vs"][:, cols, :].reshape(Ld, T, NKV, HD)
            kis[:, gb] = r["o_kisT"][:, :, cols].transpose(0, 2, 1)
            vas[:, gb] = r["o_vas"][:, cols, :]
    return (y, ys, nk, nv, nki, ks, vs, kis, vas)


def kernel(**inputs):
    cfg = FULL
    nc = build_nc(cfg)
    maps = make_in_maps(inputs, cfg, NCORES)
    res = run_bass_kernel_spmd(nc, maps, core_ids=list(range(NCORES)))
    return assemble(res.results, cfg, NCORES)
```

```python
from contextlib import ExitStack
import numpy as np
import concourse.bass as bass
import concourse.mybir as mybir
from concourse.bass_utils import run_bass_kernel_spmd

F32 = mybir.dt.float32
BF16 = mybir.dt.bfloat16
I32 = mybir.dt.int32
AF = mybir.ActivationFunctionType
ALU = mybir.AluOpType

import os as _os0
SAME_ENGINE_SYNC = _os0.environ.get('SES', '1') == '1'
EPOCH = 20000
NCORES = 8
SAVE_BF16 = True
import os as _os
DBG_STAGE = _os.environ.get('DBG_STAGE', '')


class Res:
    __slots__ = ("last_w", "readers", "dsem", "dcount")

    def __init__(self):
        self.last_w = None
        self.readers = []
        self.dsem = None
        self.dcount = 0


class Op:
    __slots__ = ("eng", "fn", "deps", "is_dma", "signal", "sem", "val", "key")

    def __init__(self, eng, fn, is_dma, key=None):
        self.eng = eng
        self.fn = fn
        self.deps = []
        self.is_dma = is_dma
        self.signal = False
        self.sem = None
        self.val = 0
        self.key = key


class Prog:
    ENGS = ("pe", "act", "dve", "pool", "sp")

    def __init__(self, nc):
        self.nc = nc
        self.ops = []
        self.sems = []
        self.out_dmas = []

    def new_sem(self, name):
        s = self.nc.alloc_semaphore(name=name)
        self.sems.append(s)
        return s

    def _add(self, op, reads, writes):
        deps = set()
        for r in reads:
            if r.last_w is not None:
                deps.add(r.last_w)
        for w in writes:
            if w.last_w is not None:
                deps.add(w.last_w)
            for rd in w.readers:
                deps.add(rd)
        deps.discard(op)
        op.deps = list(deps)
        for r in reads:
            r.readers.append(op)
        for w in writes:
            w.last_w = op
            w.readers = []
        self.ops.append(op)
        return op

    def op(self, eng, fn, reads=(), writes=()):
        return self._add(Op(eng, fn, False), list(reads), list(writes))

    def dma(self, eng, fn, reads=(), writes=(), key=None, is_out=False):
        reads = list(reads)
        writes = list(writes)
        if key is None:
            key = writes[0] if writes else reads[0]
        o = Op(eng, fn, True, key)
        self._add(o, reads, writes)
        o.signal = True
        if is_out:
            self.out_dmas.append(o)
        return o

    def emit(self, tag):
        nc = self.nc
        ops = self.ops

        def skip(p, o):
            return (not p.is_dma) and p.eng == o.eng and (not o.is_dma) and (
                p.eng == "pe" or not SAME_ENGINE_SYNC)

        for o in ops:
            for p in o.deps:
                if p.is_dma or skip(p, o):
                    continue
                p.signal = True
        eng_sems = {}
        eng_cnt = {}
        for o in ops:
            if o.is_dma:
                k = o.key
                if k.dsem is None:
                    k.dsem = self.new_sem("%sd%d" % (tag, len(self.sems)))
                k.dcount += 16
                o.sem = k.dsem
                o.val = k.dcount
            elif o.signal:
                c = eng_cnt.get(o.eng, 0)
                ep = c // EPOCH
                lst = eng_sems.setdefault(o.eng, [])
                if ep >= len(lst):
                    lst.append(self.new_sem("%se_%s_%d" % (tag, o.eng, ep)))
                o.sem = lst[ep]
                o.val = c - ep * EPOCH + 1
                eng_cnt[o.eng] = c + 1
        by_eng = {e: [] for e in self.ENGS}
        for o in ops:
            by_eng[o.eng].append(o)
        finals = {}
        for o in [o_ for o_ in ops if o_.is_dma]:
            k = id(o.sem)
            if k not in finals or finals[k][1] < o.val:
                finals[k] = (o.sem, o.val)

        def run(engine, lst, extra_final=None):
            waited = {}
            for o in lst:
                need = {}
                for p in o.deps:
                    if not p.signal or skip(p, o):
                        continue
                    k = id(p.sem)
                    if k not in need or need[k][1] < p.val:
                        need[k] = (p.sem, p.val)
                for k, (sem, val) in need.items():
                    if waited.get(k, 0) >= val:
                        continue
                    engine.wait_ge(sem, val)
                    waited[k] = val
                ins = o.fn(engine)
                if o.signal:
                    ins.then_inc(o.sem, 16 if o.is_dma else 1)
            if extra_final:
                for sem, val in extra_final:
                    if waited.get(id(sem), 0) < val:
                        engine.wait_ge(sem, val)

        with nc.Block() as block:
            @block.sync
            def _(e):
                run(e, by_eng["sp"], list(finals.values()))
            if by_eng["pe"]:
                @block.tensor
                def _(e):
                    run(e, by_eng["pe"])
            if by_eng["act"]:
                @block.scalar
                def _(e):
                    run(e, by_eng["act"])
            if by_eng["dve"]:
                @block.vector
                def _(e):
                    run(e, by_eng["dve"])
            if by_eng["pool"]:
                @block.gpsimd
                def _(e):
                    run(e, by_eng["pool"])
        nc.clear_and_free_semaphores(self.sems)
        nc.all_engine_barrier()


D = 1024
DC = 8
DFF = 2816
FC = 22
DIN = 6216
NH = 8
NKV = 2
HD = 128
IH = 8
ID = 64
C_U, C_V, C_Q, C_K, C_VV, C_QI, C_KI, C_WI, C_GA, C_GB = 0, 1024, 2048, 3072, 3328, 3584, 4096, 4160, 4168, 5192
EPS = 1e-6
NEG = -1.0e30
MB = -30000.0
BIS_ITERS = 22
BIS_LO = -16.0
BIS_W = 32.0


class Cfg:
    def __init__(self, nseq, S, ksel, ndec, tdec, npg, pool, depth=2, ksel_s=256):
        self.nseq, self.S, self.ksel, self.ksel_s = nseq, S, ksel, ksel_s
        self.ndec, self.tdec, self.npg, self.pool, self.depth = ndec, tdec, npg, pool, depth


FULL = Cfg(2, 2048, 256, 4, 4, 128, 5120)


class Tl:
    def __init__(self, t, nres=1):
        self.t = t
        self.r = [Res() for _ in range(nres)]


class K:
    def __init__(self, nc, P, stack, pfx=""):
        self.nc, self.P, self.stack, self.pfx = nc, P, stack, pfx
        self.wslot_i = 0

    def sb(self, name, shape, dt, nres=1):
        t = self.stack.enter_context(self.nc.sbuf_tensor(self.pfx + name, list(shape), dt))
        return Tl(t, nres)

    def ps(self, name, shape, dt=F32, nres=1):
        t = self.stack.enter_context(self.nc.psum_tensor(self.pfx + name, list(shape), dt))
        return Tl(t, nres)

    def MM(self, out, lhsT, rhs, start, stop, R, W):
        self.P.op("pe", lambda e: e.matmul(out, lhsT=lhsT, rhs=rhs, start=start, stop=stop), R, W)

    def TR(self, out, in_, ident, R, W):
        self.P.op("pe", lambda e: e.transpose(out=out, in_=in_, identity=ident), R, W)

    def ACT(self, out, in_, func, R, W, bias=None, scale=None):
        kw = {}
        if bias is not None:
            kw["bias"] = bias
        if scale is not None:
            kw["scale"] = scale
        self.P.op("act", lambda e: e.activation(out=out, in_=in_, func=func, **kw), R, W)

    def TS(self, eng, out, in0, s1, s2, op0, op1, R, W, accum_out=None):
        kw = {}
        if op1 is not None:
            kw["op1"] = op1
        if accum_out is not None:
            kw["accum_out"] = accum_out
        self.P.op(eng, lambda e: e.tensor_scalar(out=out, in0=in0, scalar1=s1, scalar2=s2, op0=op0, **kw), R, W)

    def TT(self, eng, out, in0, in1, op, R, W):
        self.P.op(eng, lambda e: e.tensor_tensor(out=out, in0=in0, in1=in1, op=op), R, W)

    def STT(self, out, in0, scalar, in1, op0, op1, R, W):
        self.P.op("dve", lambda e: e.scalar_tensor_tensor(out=out, in0=in0, scalar=scalar, in1=in1, op0=op0, op1=op1), R, W)

    def CP(self, eng, out, in_, R, W):
        if eng == "act":
            self.P.op("act", lambda e: e.copy(out=out, in_=in_), R, W)
        elif eng == "dve":
            self.P.op("dve", lambda e: e.tensor_scalar(out=out, in0=in_, scalar1=1.0, scalar2=None, op0=ALU.mult), R, W)
        else:
            self.P.op(eng, lambda e: e.tensor_copy(out=out, in_=in_), R, W)

    def MS(self, eng, ap, val, W):
        self.P.op(eng, lambda e: e.memset(ap, val), [], W)

    def DMA(self, eng, out, in_, R, W, is_out=False, key=None):
        self.P.dma(eng, lambda e: e.dma_start(out=out, in_=in_), R, W, key=key, is_out=is_out)


class Dense:
    def __init__(self, k, NT, dr, consts, wmode="cast", out_q="sp", nslots=3):
        self.k, self.NT, self.dr, self.c = k, NT, dr, consts
        self.wmode, self.out_q = wmode, out_q
        self.QITz = None
        NTT = NT // 128
        self.NTT = NTT
        k_ = k
        self.x = k_.sb("x", [128, DC, NT], F32, DC)
        self.h = k_.sb("h", [128, DC, NT], BF16, DC)
        self.sq = k_.sb("sq", [128, 2, NT], BF16, 2)
        self.rstd = k_.sb("rstd", [128, NT], F32)
        hidraw = k_.sb("hidraw", [128, FC * NT // 2], F32, FC)
        self.hidraw = hidraw
        self.hid = Tl(hidraw.t[:].bitcast(BF16).rearrange("p (c n) -> p c n", c=FC), 0)
        self.hid.r = hidraw.r
        self.tmpf = k_.sb("tmpf", [128, 3, 512], F32, 3)
        self.tmp_i = 0
        self.nslots = nslots
        self.wring = k_.sb("wring", [128, nslots, 4096], BF16, nslots)
        self.wsave = [Res() for _ in range(nslots)]
        self.banks = [k_.ps("bank%d" % i, [128, 512]) for i in range(8)]
        self.bank_i = 0
        self.eps = k_.sb("eps", [128, 1], F32)
        k_.MS("pool", self.eps.t[:], EPS, self.eps.r)
        self.QT = k_.sb("QT", [128, NH, NT], BF16, NH)
        self.QIT = k_.sb("QIT", [64, IH, NT], BF16, IH)
        self.aT = k_.sb("aT", [128, DC, NT], BF16, DC)
        self.attnT = k_.sb("attnT", [128, NH, NT], BF16, NH)
        self.mT = self.QT
        self.wtok = k_.sb("wtok", [128, NTT, IH], F32, NTT)
        self.vg = k_.sb("vg", [128, D], F32)
        self.vn = k_.sb("vn", [128, 1, D], BF16, 1)
        self.vnf = k_.sb("vnf", [128, D], F32) if NT == 128 else None
        self.vjunk = k_.sb("vjunk", [128, D], BF16)
        self.sm = k_.sb("sm", [128, 8], F32, 4)
        self.vstage = k_.sb("vstage", [128, 1, NKV * HD], F32, 1)
        self.kstage = k_.sb("kstage", [128, 1, NT], F32, 1)

    def bank(self, group=(0, 1, 2, 3)):
        b = self.banks[group[self.bank_i % len(group)]]
        self.bank_i += 1
        return b

    def tmp(self):
        i = self.tmp_i % 3
        self.tmp_i += 1
        return self.tmpf.t[:, i, 0:self.NT], self.tmpf.r[i]

    def wload(self, w2d, KC, c0, nc_):
        k = self.k
        i = k.wslot_i % self.nslots
        k.wslot_i += 1
        view = self.wring.t[:, i, 0:KC * nc_].rearrange("p (k n) -> p k n", k=KC)
        src = w2d.rearrange("(k p) n -> p k n", p=128)[:, :, c0:c0 + nc_]
        kk, nn = w2d.shape
        L = w2d.offset // (kk * nn)
        bf = self.dr[w2d.tensor.name + "_bf"][L].rearrange("(k p) n -> p k n", p=128)[:, :, c0:c0 + nc_]
        wr = self.wring.r[i]
        parts = [(0, KC)] if KC <= 8 else [(0, KC // 2), (KC // 2, KC)]
        for (a0, a1) in parts:
            if self.wmode == "bf16":
                k.DMA("sp", view[:, a0:a1, :], bf[:, a0:a1, :], [], [wr])
            else:
                k.DMA("pool", view[:, a0:a1, :], src[:, a0:a1, :], [], [wr])
        if self.wmode == "save":
            for (a0, a1) in parts:
                k.DMA("sp", bf[:, a0:a1, :], view[:, a0:a1, :], [wr], [], is_out=True, key=self.wsave[i])
        return view, self.wring.r[i]

    def rmsnorm(self, gT):
        k, NT = self.k, self.NT
        st = self.bank((4,))
        for c in range(DC):
            j = c % 2
            k.ACT(self.sq.t[:, j, :], self.x.t[:, c, :], AF.Square, [self.x.r[c]], [self.sq.r[j]])
            k.MM(st.t[:, 0:NT], self.c["ones"].t[:], self.sq.t[:, j, :], c == 0, c == DC - 1, [self.sq.r[j], self.c["ones"].r[0]], st.r)
        k.ACT(self.rstd.t[:], st.t[:, 0:NT], AF.Ln, st.r + self.eps.r, self.rstd.r, bias=self.eps.t[:], scale=1.0 / D)
        k.ACT(self.rstd.t[:], self.rstd.t[:], AF.Exp, self.rstd.r, self.rstd.r, scale=-0.5)
        for c in range(DC):
            k.STT(self.h.t[:, c, :], self.x.t[:, c, :], gT.t[:, c:c + 1], self.rstd.t[:], ALU.mult, ALU.mult,
                  [self.x.r[c], gT.r[0], self.rstd.r[0]], [self.h.r[c]])

    def proj(self, wview, wres, col, M, src, KC, ps_ap, ps_res):
        k = self.k
        for kc in range(KC):
            k.MM(ps_ap, wview[:, kc, col:col + M], src.t[:, kc, :], kc == 0, kc == KC - 1,
                 [wres, src.r[kc]], ps_res)

    def ffn(self, gT, w_up, w_down):
        k, NT = self.k, self.NT
        self.rmsnorm(gT)
        for j0 in range(0, FC, 4):
            n = min(4, FC - j0)
            gv, gr = self.wload(w_up, DC, j0 * 128, n * 128)
            uv, ur = self.wload(w_up, DC, DFF + j0 * 128, n * 128)
            for i in range(n):
                pg = self.bank()
                self.proj(gv, gr, i * 128, 128, self.h, DC, pg.t[:, 0:NT], pg.r)
                pu = self.bank()
                self.proj(uv, ur, i * 128, 128, self.h, DC, pu.t[:, 0:NT], pu.r)
                ta, tr = self.tmp()
                k.ACT(ta, pg.t[:, 0:NT], AF.Silu, pg.r, [tr])
                k.TT("dve", self.hid.t[:, j0 + i, :], pu.t[:, 0:NT], ta, ALU.mult, pu.r + [tr], [self.hid.r[j0 + i]])
        if DBG_STAGE == "ffnup":
            return
        for c in range(DC):
            dv, drr = self.wload(w_down, FC, c * 128, 128)
            if True:
                pd = self.bank()
                self.proj(dv, drr, 0, 128, self.hid, FC, pd.t[:, 0:NT], pd.r)
                k.STT(self.x.t[:, c, :], pd.t[:, 0:NT], 0.5, self.x.t[:, c, :], ALU.mult, ALU.add,
                      pd.r + [self.x.r[c]], [self.x.r[c]])

    def headnorm(self, ps, g, out_ap, out_res, stage=None):
        k, NT = self.k, self.NT
        k.ACT(self.sq.t[:, 0, :], ps.t[:, 0:NT], AF.Square, ps.r, [self.sq.r[0]])
        st = self.bank((4,))
        k.MM(st.t[:, 0:NT], self.c["ones"].t[:], self.sq.t[:, 0, :], True, True, [self.sq.r[0], self.c["ones"].r[0]], st.r)
        rs, rr = self.tmp()
        k.ACT(rs, st.t[:, 0:NT], AF.Ln, st.r + self.eps.r, [rr], bias=self.eps.t[:], scale=1.0 / HD)
        k.ACT(rs, rs, AF.Exp, [rr], [rr], scale=-0.5)
        k.STT(out_ap, ps.t[:, 0:NT], g.t[:, 0:1], rs, ALU.mult, ALU.mult, ps.r + [g.r[0], rr], out_res)
        if stage is not None:
            sap, sres = stage
            k.STT(sap, ps.t[:, 0:NT], g.t[:, 0:1], rs, ALU.mult, ALU.mult, ps.r + [g.r[0], rr], sres)

    def mixer_dense(self, L, kv, tok0, out_k, out_v, out_ki, out_va=None, sample=False):
        k, NT, NTT, c, dr = self.k, self.NT, self.NTT, self.c, self.dr
        w_in = dr["w_in"][L]
        self.rmsnorm(c["g_mix"][L])
        k.DMA("sp", c["gv"][L].t[:], dr["g_v"][L], [], c["gv"][L].r)
        kt0 = tok0 // 128
        wv0, wr0 = self.wload(w_in, DC, C_V, 512)
        wv1, wr1 = self.wload(w_in, DC, C_V + 512, 512)
        for tt in range(NTT):
            for half, (wv, wr) in enumerate(((wv0, wr0), (wv1, wr1))):
                pv = self.bank()
                for kc in range(DC):
                    k.MM(pv.t[:, :], self.h.t[:, kc, tt * 128:(tt + 1) * 128], wv[:, kc, :], kc == 0, kc == DC - 1,
                         [self.h.r[kc], wr], pv.r)
                k.ACT(self.vg.t[:, half * 512:(half + 1) * 512], pv.t[:, :], AF.Gelu_apprx_tanh, pv.r, self.vg.r)
            ss = self.sm.t[:, 0:1]
            k.P.op("act", lambda e, ss=ss: e.activation(out=self.vjunk.t[:], in_=self.vg.t[:], func=AF.Square, accum_out=ss),
                   self.vg.r, [self.vjunk.r[0], self.sm.r[0]])
            k.ACT(self.sm.t[:, 1:2], ss, AF.Ln, [self.sm.r[0]] + self.eps.r, [self.sm.r[1]], bias=self.eps.t[:], scale=1.0 / D)
            k.ACT(self.sm.t[:, 1:2], self.sm.t[:, 1:2], AF.Exp, [self.sm.r[1]], [self.sm.r[1]], scale=-0.5)
            j = 0
            k.STT(self.vn.t[:, j, :], self.vg.t[:], self.sm.t[:, 1:2], c["gv"][L].t[:], ALU.mult, ALU.mult,
                  self.vg.r + [self.sm.r[1], c["gv"][L].r[0]], [self.vn.r[j]])
            if out_va is not None:
                k.STT(self.vnf.t[:], self.vg.t[:], self.sm.t[:, 1:2], c["gv"][L].t[:], ALU.mult, ALU.mult,
                      self.vg.r + [self.sm.r[1], c["gv"][L].r[0]], self.vnf.r)
                out_va(self.vnf)
            WT = c["WTs"][L] if sample else c["WT"][L]
            for g in range(8):
                psv = self.bank()
                k.MM(psv.t[:, 0:128], self.vn.t[:, j, g * 128:(g + 1) * 128], WT.t[:, g, :], True, False,
                     [self.vn.r[j], WT.r[0]], psv.r)
                k.MM(psv.t[:, 0:128], c["ones1"].t[0:1, :], c["bhi"][L].t[0:1, g, :], False, False,
                     [c["ones1"].r[0], c["bhi"][L].r[0]], psv.r)
                k.MM(psv.t[:, 0:128], c["ones1"].t[0:1, :], c["blo"][L].t[0:1, g, :], False, True,
                     [c["ones1"].r[0], c["blo"][L].r[0]], psv.r)
                k.CP("act", self.attnT.t[:, g, tt * 128:(tt + 1) * 128], psv.t[:, 0:128], psv.r, [self.attnT.r[g]])
        if DBG_STAGE == "md1":
            return
        for hb in range(2):
            wv, wr = self.wload(w_in, DC, C_U + hb * 512, 512)
            for i in range(4):
                g = hb * 4 + i
                pu = self.bank()
                self.proj(wv, wr, i * 128, 128, self.h, DC, pu.t[:, 0:NT], pu.r)
                ta, tr = self.sq.t[:, g % 2, :], self.sq.r[g % 2]
                k.ACT(ta, pu.t[:, 0:NT], AF.Gelu_apprx_tanh, pu.r, [tr])
                if DBG_STAGE != "md2a":
                    k.TT("dve", self.aT.t[:, g, :], self.attnT.t[:, g, :], ta, ALU.mult, [self.attnT.r[g], tr], [self.aT.r[g]])
        if DBG_STAGE in ("md2", "md2a"):
            return
        for hb in range(2):
            wv, wr = self.wload(w_in, DC, C_Q + hb * 512, 512)
            for i in range(4):
                hh = hb * 4 + i
                pq = self.bank()
                self.proj(wv, wr, i * 128, 128, self.h, DC, pq.t[:, 0:NT], pq.r)
                self.headnorm(pq, c["gq"][L], self.QT.t[:, hh, :], [self.QT.r[hh]])
        if DBG_STAGE == "md3":
            return
        wv, wr = self.wload(w_in, DC, C_K, 512)
        for kh in range(NKV):
            pk = self.bank()
            self.proj(wv, wr, kh * 128, 128, self.h, DC, pk.t[:, 0:NT], pk.r)
            j = 0
            kres = [kv.KTr[kt0 + t] for t in range(NTT)]
            self.headnorm(pk, c["gk"][L], kv.KT.t[:, kh, tok0:tok0 + NT], kres,
                          stage=(self.kstage.t[:, j, :], [self.kstage.r[j]]))
            out_k(kh, self.kstage.t[:, j, :], self.kstage.r[j])
        for tt in range(NTT):
            pv = self.bank()
            for kc in range(DC):
                k.MM(pv.t[:, 0:256], self.h.t[:, kc, tt * 128:(tt + 1) * 128], wv[:, kc, 256:512], kc == 0, kc == DC - 1,
                     [self.h.r[kc], wr], pv.r)
            j = 0
            k.CP("act", self.vstage.t[:, j, :], pv.t[:, 0:256], pv.r, [self.vstage.r[j]])
            k.CP("pool", kv.V.t[:, kt0 + tt, :, :].rearrange("p a b -> p (a b)"), self.vstage.t[:, j, :], [self.vstage.r[j]], [kv.Vr[kt0 + tt]])
            out_v(tt, self.vstage.t[:, j, :], self.vstage.r[j])
        if DBG_STAGE == "md4":
            return
        wv, wr = self.wload(w_in, DC, C_QI, 512)
        for hh in range(IH):
            pq = self.bank()
            self.proj(wv, wr, hh * 64, 64, self.h, DC, pq.t[0:64, 0:NT], pq.r)
            k.ACT(self.QIT.t[:, hh, :], pq.t[0:64, 0:NT], AF.Copy, pq.r, [self.QIT.r[hh]], scale=ID ** -0.5)
            if self.QITz is not None:
                for b_ in range(4):
                    k.ACT(self.QITz.t[:, b_, hh, 32 * b_:32 * b_ + 32], pq.t[0:64, 32 * b_:32 * b_ + 32], AF.Copy, pq.r, self.QITz.r,
                          scale=ID ** -0.5)
        wv, wr = self.wload(w_in, DC, C_KI, 72)
        pk = self.bank()
        self.proj(wv, wr, 0, 64, self.h, DC, pk.t[0:64, 0:NT], pk.r)
        k.CP("act", self.kstage.t[0:64, 0, :], pk.t[0:64, 0:NT], pk.r, [self.kstage.r[0]])
        k.CP("pool", kv.KIT.t[:, tok0:tok0 + NT], self.kstage.t[0:64, 0, :], [self.kstage.r[0]], [kv.KIr[kt0 + t] for t in range(NTT)])
        out_ki(self.kstage.t[0:64, 0, :], self.kstage.r[0])
        for tt in range(NTT):
            pw = self.bank()
            for kc in range(DC):
                k.MM(pw.t[:, 0:8], self.h.t[:, kc, tt * 128:(tt + 1) * 128], wv[:, kc, 64:72], kc == 0, kc == DC - 1,
                     [self.h.r[kc], wr], pw.r)
            k.ACT(self.wtok.t[:, tt, :], pw.t[:, 0:8], AF.Copy, pw.r, [self.wtok.r[tt]], scale=IH ** -0.5)

    def index_scores(self, kv, tt, nk, scores, prange=(0, 128), col0=0):
        k = self.k
        p0, p1 = prange
        for ch in range((nk + 511) // 512):
            w = min(512, nk - ch * 512)
            kts = [kv.KIr[t] for t in range(ch * 4, ch * 4 + (w + 127) // 128)]
            for hh in range(IH):
                pd = self.bank((0, 1))
                k.MM(pd.t[:, 0:w], self.QIT.t[:, hh, tt * 128:(tt + 1) * 128], kv.KIT.t[:, ch * 512:ch * 512 + w], True, True,
                     [self.QIT.r[hh]] + kts, pd.r)
                ti = self.tmp_i % 3
                self.tmp_i += 1
                ta, tr = self.tmpf.t[:, ti, :], self.tmpf.r[ti]
                k.ACT(ta[p0:p1, 0:w], pd.t[p0:p1, 0:w], AF.Relu, pd.r, [tr])
                sc = scores.t[p0:p1, col0 + ch * 512:col0 + ch * 512 + w]
                if hh == 0:
                    k.TS("dve", sc, ta[p0:p1, 0:w], self.wtok.t[p0:p1, tt, 0:1], None, ALU.mult, None,
                         [tr, self.wtok.r[tt]], scores.r)
                else:
                    k.STT(sc, ta[p0:p1, 0:w], self.wtok.t[p0:p1, tt, hh:hh + 1], sc, ALU.mult, ALU.add,
                          [tr, self.wtok.r[tt]] + scores.r, scores.r)

    def index_scores_pe(self, kv, tt, nk, scores, diag):
        k = self.k
        for hh in range(IH):
            k.TS("dve", diag.t[:, hh, :], self.c["I4"].t[:, 0, :], self.wtok.t[:, tt, hh:hh + 1], None, ALU.mult, None,
                 [self.c["I4"].r[0], self.wtok.r[tt]], diag.r)
        for ch in range((nk + 511) // 512):
            w = min(512, nk - ch * 512)
            kts = [kv.KIr[t] for t in range(ch * 4, ch * 4 + (w + 127) // 128)]
            psc = self.banks[3]
            pend = None
            for hh in range(IH):
                pd = self.banks[hh % 2]
                k.MM(pd.t[:, 0:w], self.QIT.t[:, hh, tt * 128:(tt + 1) * 128], kv.KIT.t[:, ch * 512:ch * 512 + w], True, True,
                     [self.QIT.r[hh]] + kts, pd.r)
                j = hh % 2
                k.ACT(self.sq.t[:, j, 0:w], pd.t[:, 0:w], AF.Relu, pd.r, [self.sq.r[j]])
                if pend is not None:
                    ph, pj = pend
                    k.MM(psc.t[:, 0:w], diag.t[:, ph, :], self.sq.t[:, pj, 0:w], ph == 0, False, diag.r + [self.sq.r[pj]], psc.r)
                pend = (hh, j)
            ph, pj = pend
            k.MM(psc.t[:, 0:w], diag.t[:, ph, :], self.sq.t[:, pj, 0:w], False, True, diag.r + [self.sq.r[pj]], psc.r)
            k.CP("act", scores.t[:, ch * 512:ch * 512 + w], psc.t[:, 0:w], psc.r, scores.r)

    def bisect_mask(self, scores, ncol, ksel, maskb, junk, bis):
        k = self.k
        lo, mid, g = bis.t[:, 0:1], bis.t[:, 1:2], bis.t[:, 3:4]
        cnt = bis.t[:, 2:3]
        cnt8 = bis.t[:, 8:32]
        JW = junk.t.shape[1]
        nchunk = (ncol + JW - 1) // JW
        W = BIS_W
        k.MS("dve", mid, BIS_LO + 0.5 * W, bis.r)
        for it in range(BIS_ITERS):
            for ci in range(nchunk):
                w = min(JW, ncol - ci * JW)
                k.TS("dve", junk.t[:, 0:w], scores.t[:, ci * JW:ci * JW + w], mid, 0.0, ALU.is_ge, ALU.add,
                     scores.r + bis.r, junk.r + bis.r, accum_out=cnt8[:, ci:ci + 1])
            if nchunk > 1:
                k.P.op("dve", lambda e: e.reduce_sum(out=cnt, in_=cnt8[:, 0:nchunk], axis=mybir.AxisListType.X), bis.r, bis.r)
                cc = cnt
            else:
                cc = cnt8[:, 0:1]
            if it < BIS_ITERS - 1:
                k.TS("dve", g, cc, ksel - 0.5, 0.5 * W, ALU.is_ge, ALU.mult, bis.r, bis.r)
                k.STT(mid, g, -0.25 * W, mid, ALU.add, ALU.add, bis.r, bis.r)
            else:
                k.TS("dve", g, cc, ksel - 0.5, 0.5 * W, ALU.is_ge, ALU.mult, bis.r, bis.r)
                k.STT(lo, g, -0.5 * W, mid, ALU.add, ALU.add, bis.r, bis.r)
            W *= 0.5
        if maskb is not None:
            k.TS("dve", maskb.t[:, 0:ncol], scores.t[:, 0:ncol], lo, MB, ALU.is_lt, ALU.mult, scores.r + bis.r, maskb.r)

    def attend(self, kv, tt, ntile, maskb, mcol0, nslot, slot0, first, last, acc):
        k = self.k
        OT, RS = acc
        N = 4 * nslot
        q0 = tt * 128 + slot0
        pend = None

        def flush(p):
            kh, kt, pT, pr, st, sp = p
            k.MM(OT[kh].t[:, 0:N], kv.V.t[:, kt, kh, :], pT, st, sp, [kv.Vr[kt], pr], OT[kh].r)
            k.MM(RS[kh].t[:, 0:N], self.c["ones"].t[:], pT, st, sp, [self.c["ones"].r[0], pr], RS[kh].r)

        for kh in range(NKV):
            for kt in range(ntile):
                pS = self.bank((2, 3))
                outv = pS.t[:, 0:N].rearrange("p (g t) -> p g t", g=4)
                k.MM(outv, kv.KT.t[:, kh, kt * 128:(kt + 1) * 128], self.QT.t[:, kh * 4:(kh + 1) * 4, q0:q0 + nslot], True, False,
                     [kv.KTr[kt]] + self.QT.r[kh * 4:(kh + 1) * 4], pS.r)
                k.MM(outv, maskb.t[:, mcol0 + kt * 128:mcol0 + (kt + 1) * 128], self.c["I4"].t[:, :, slot0:slot0 + nslot], False, True,
                     maskb.r + [self.c["I4"].r[0]], pS.r)
                i = self.tmp_i % 3
                self.tmp_i += 1
                pT = self.c["pT"].t[:, i, 0:N]
                pr = self.c["pT"].r[i]
                k.ACT(pT, pS.t[:, 0:N], AF.Exp, pS.r, [pr], scale=HD ** -0.5)
                if pend is not None:
                    flush(pend)
                pend = (kh, kt, pT, pr, first and kt == 0, last and kt == ntile - 1)
        if pend is not None:
            flush(pend)

    def attn_finish(self, tt, nslot, slot0, acc):
        k = self.k
        OT, RS = acc
        N = 4 * nslot
        for kh in range(NKV):
            ta, tr = self.tmp()
            k.ACT(ta[:, 0:N], RS[kh].t[:, 0:N], AF.Ln, RS[kh].r, [tr])
            k.ACT(ta[:, 0:N], ta[:, 0:N], AF.Exp, [tr], [tr], scale=-1.0)
            q0 = tt * 128 + slot0
            k.TT("dve", self.attnT.t[:, kh * 4:(kh + 1) * 4, q0:q0 + nslot], OT[kh].t[:, 0:N].rearrange("p (g t) -> p g t", g=4),
                 ta[:, 0:N].rearrange("p (g t) -> p g t", g=4), ALU.mult, OT[kh].r + [tr], self.attnT.r[kh * 4:(kh + 1) * 4])

    def merge_out(self, L):
        k, NT, dr = self.k, self.NT, self.dr
        w_in = dr["w_in"][L]
        for hb in range(2):
            wa, ra = self.wload(dr["w_pa"][L], DC, hb * 512, 512)
            wg, rg = self.wload(w_in, DC, C_GA + hb * 512, 512)
            for i in range(4):
                c = hb * 4 + i
                pa = self.bank()
                self.proj(wa, ra, i * 128, 128, self.aT, DC, pa.t[:, 0:NT], pa.r)
                pg = self.bank()
                self.proj(wg, rg, i * 128, 128, self.h, DC, pg.t[:, 0:NT], pg.r)
                ta, tr = self.tmp()
                k.ACT(ta, pg.t[:, 0:NT], AF.Sigmoid, pg.r, [tr])
                k.TT("dve", self.hid.t[:, c, :], pa.t[:, 0:NT], ta, ALU.mult, pa.r + [tr], [self.hid.r[c]])
        for hb in range(2):
            wb, rb = self.wload(dr["w_pb"][L], DC, hb * 512, 512)
            wg, rg = self.wload(w_in, DC, C_GB + hb * 512, 512)
            for i in range(4):
                c = hb * 4 + i
                pb = self.bank()
                self.proj(wb, rb, i * 128, 128, self.attnT, DC, pb.t[:, 0:NT], pb.r)
                pg = self.bank()
                self.proj(wg, rg, i * 128, 128, self.h, DC, pg.t[:, 0:NT], pg.r)
                ta, tr = self.tmp()
                k.ACT(ta, pg.t[:, 0:NT], AF.Sigmoid, pg.r, [tr])
                tb, trb = self.sq.t[:, c % 2, :], self.sq.r[c % 2]
                k.TT("dve", tb, pb.t[:, 0:NT], ta, ALU.mult, pb.r + [tr], [trb])
                k.TT("dve", self.mT.t[:, c, :], tb, self.hid.t[:, c, :], ALU.add, [trb, self.hid.r[c]], [self.mT.r[c]])
        for hb in range(2):
            wo, ro = self.wload(dr["w_out"][L], DC, hb * 512, 512)
            for i in range(4):
                c = hb * 4 + i
                po = self.bank()
                self.proj(wo, ro, i * 128, 128, self.mT, DC, po.t[:, 0:NT], po.r)
                k.TT("dve", self.x.t[:, c, :], po.t[:, 0:NT], self.x.t[:, c, :], ALU.add, po.r + [self.x.r[c]], [self.x.r[c]])


class KVCache:
    def __init__(self, k, name, nkeys, with_kit=True):
        nt = nkeys // 128
        self.KT = k.sb(name + "KT", [128, NKV, nkeys], BF16)
        self.V = k.sb(name + "V", [128, nt, NKV, HD], BF16)
        self.KIT = k.sb(name + "KIT", [64, nkeys], BF16) if with_kit else None
        self.KTr = [Res() for _ in range(nt)]
        self.Vr = [Res() for _ in range(nt)]
        self.KIr = [Res() for _ in range(nt)]


def load_consts(k, dr, cfg, sample):
    c = {}
    Lr = range(cfg.depth)
    c["ones"] = k.sb("ones", [128, 128], BF16)
    k.MS("pool", c["ones"].t[:], 1.0, c["ones"].r)
    c["ones1"] = k.sb("ones1", [1, 128], BF16)
    k.MS("pool", c["ones1"].t[:], 1.0, c["ones1"].r)
    c["I4"] = k.sb("I4", [128, 4, 128], BF16)
    k.DMA("pool", c["I4"].t[:], dr["I4"], [], c["I4"].r)
    c["tri"] = k.sb("tri", [128, 128], F32)
    k.DMA("sp", c["tri"].t[:], dr["tri"], [], c["tri"].r)
    triu = k.sb("triu", [128, 128], BF16)
    k.DMA("pool", triu.t[:], dr["triu"], [], triu.r)
    c["pT"] = k.sb("pT", [128, 3, 512], BF16, 3)
    for nm in ("g_ffn1", "g_mix", "g_ffn2"):
        c[nm] = []
        for L in Lr:
            t = k.sb("%s%d" % (nm, L), [128, DC], F32)
            k.DMA("sp", t.t[:], dr[nm][L], [], t.r)
            c[nm].append(t)
    for nm, src in (("gq", "g_q"), ("gk", "g_k")):
        c[nm] = []
        for L in Lr:
            t = k.sb("%s%d" % (nm, L), [128, 1], F32)
            k.DMA("sp", t.t[:], dr[src][L], [], t.r)
            c[nm].append(t)
    gvt = k.sb("gvt", [128, D], F32)
    c["gv"] = [gvt for L in Lr]
    c["bhi"], c["blo"], c["WT"], c["WTs"] = [], [], [], []
    bf = k.sb("bf", [1, 8, 128], F32)
    bh32 = k.sb("bh32", [1, 8, 128], F32)
    for L in Lr:
        if sample:
            for b in range(4):
                k.DMA("sp", bf.t[0:1, :, 32 * b:32 * b + 32], dr["b_s"][L][:, :, 0:32], [], bf.r)
        else:
            k.DMA("sp", bf.t[:], dr["b_s"][L], [], bf.r)
        bhi = k.sb("bhi%d" % L, [1, 8, 128], BF16)
        blo = k.sb("blo%d" % L, [1, 8, 128], BF16)
        k.CP("dve", bhi.t[:], bf.t[:], bf.r, bhi.r)
        k.CP("dve", bh32.t[:], bhi.t[:], bhi.r, bh32.r)
        k.TT("dve", blo.t[:], bf.t[:], bh32.t[:], ALU.subtract, bf.r + bh32.r, blo.r)
        c["bhi"].append(bhi)
        c["blo"].append(blo)
        WT = k.sb("WT%d" % L, [128, 8, 128], BF16)
        if sample:
            k.MS("pool", WT.t[:], 0.0, WT.r)
            for b in range(4):
                k.DMA("pool", WT.t[32 * b:32 * b + 4, :, 32 * b:32 * b + 4],
                      dr["wsT"][L][:, 0:4, 0:4].rearrange("g s t -> s g t"), [], WT.r)
        else:
            k.DMA("pool", WT.t[:], dr["wsT"][L].rearrange("g s t -> s g t"), [], WT.r)
        for g in range(8):
            k.TT("dve", WT.t[:, g, :], triu.t[:], WT.t[:, g, :], ALU.mult, WT.r + triu.r, WT.r)
        c["WTs" if sample else "WT"].append(WT)
    return c


def build_prompt(nc, dr, cfg):
    NT = 512
    with ExitStack() as stack:
        P = Prog(nc)
        k = K(nc, P, stack, "p_")
        c = load_consts(k, dr, cfg, False)
        dn = Dense(k, NT, dr, c, wmode=("bf16" if SAVE_BF16 else "cast"), out_q=("pool" if SAVE_BF16 else "sp"))
        OQ = dn.out_q
        S = cfg.S
        caches = [KVCache(k, "c%d" % L, S) for L in range(cfg.depth)]
        scores = [k.sb("scores%d" % i, [128, S], F32) for i in range(1)]
        sc2 = Tl(dn.hidraw.t[:, 2048:2048 + S], 0)
        sc2.r = dn.hidraw.r[8:16]
        scores.append(sc2)
        maskb = [k.sb("maskb%d" % i, [128, S], BF16) for i in range(1)]
        junk = Tl(dn.hidraw.t[:, 4096:4096 + S // 2].bitcast(BF16), 0)
        junk.r = dn.hidraw.r[16:20]
        diag = k.sb("diag", [128, IH, 128], BF16)
        bis = k.sb("bis", [128, 32], F32)
        OT = [dn.banks[4], dn.banks[5]]
        RS = [dn.banks[6], dn.banks[7]]
        qn = 0
        for s in range(cfg.nseq):
            for blk in range(S // NT):
                t0 = blk * NT
                for c_ in range(DC):
                    k.DMA(OQ, dn.x.t[:, c_, :], dr["xT"][s, c_ * 128:(c_ + 1) * 128, t0:t0 + NT], [], [dn.x.r[c_]])
                for L in range(cfg.depth):
                    kv = caches[L]
                    if DBG_STAGE == "none":
                        continue
                    if DBG_STAGE == "norm":
                        dn.rmsnorm(c["g_ffn1"][L])
                        continue
                    dn.ffn(c["g_ffn1"][L], dr["w_up1"][L], dr["w_down1"][L])
                    if DBG_STAGE in ("ffn", "ffnup"):
                        continue

                    def out_k(kh, ap, res, L=L, s=s, t0=t0):
                        k.DMA(OQ, dr["o_kT"][L, s, kh * 128:(kh + 1) * 128, t0:t0 + NT], ap, [res], [], is_out=True)

                    def out_v(tt, ap, res, L=L, s=s, t0=t0):
                        k.DMA(OQ, dr["o_v"][L, s, t0 + tt * 128:t0 + (tt + 1) * 128, :], ap, [res], [], is_out=True)

                    def out_ki(ap, res, L=L, s=s, t0=t0):
                        k.DMA(OQ, dr["o_kiT"][L, s, :, t0:t0 + NT], ap, [res], [], is_out=True)

                    dn.mixer_dense(L, kv, t0, out_k, out_v, out_ki)
                    if DBG_STAGE in ("dense", "md1", "md2", "md2a", "md3", "md4"):
                        continue
                    def idx(tt, kv=kv, blk=blk):
                        qg = blk * 4 + tt
                        nk = (qg + 1) * 128
                        sc = scores[tt % 2]
                        dn.index_scores_pe(kv, tt, nk, sc, diag)
                        k.TT("dve", sc.t[:, nk - 128:nk], c["tri"].t[:], sc.t[:, nk - 128:nk], ALU.add, sc.r + c["tri"].r, sc.r)

                    def bisect(tt, blk=blk):
                        qg = blk * 4 + tt
                        nk = (qg + 1) * 128
                        if nk > cfg.ksel:
                            dn.bisect_mask(scores[tt % 2], nk, cfg.ksel, None, junk, bis)

                    def mask(tt, blk=blk):
                        qg = blk * 4 + tt
                        nk = (qg + 1) * 128
                        sc, mb = scores[tt % 2], maskb[0]
                        if nk > cfg.ksel:
                            k.TS("dve", mb.t[:, 0:nk], sc.t[:, 0:nk], bis.t[:, 0:1], MB, ALU.is_lt, ALU.mult, sc.r + bis.r, mb.r)
                        else:
                            k.TS("dve", mb.t[:, 0:nk], sc.t[:, 0:nk], -1.0e29, MB, ALU.is_lt, ALU.mult, sc.r, mb.r)

                    def att(tt, kv=kv, blk=blk):
                        qg = blk * 4 + tt
                        dn.attend(kv, tt, qg + 1, maskb[0], 0, 128, 0, True, True, (OT, RS))
                        dn.attn_finish(tt, 128, 0, (OT, RS))

                    nq = NT // 128
                    idx(0)
                    bisect(0)
                    mask(0)
                    for tt in range(nq):
                        if tt + 1 < nq:
                            idx(tt + 1)
                            bisect(tt + 1)
                        att(tt)
                        if tt + 1 < nq:
                            mask(tt + 1)
                    dn.merge_out(L)
                    dn.ffn(c["g_ffn2"][L], dr["w_up2"][L], dr["w_down2"][L])
                for c_ in range(DC):
                    k.DMA(OQ, dr["o_yT"][s, c_ * 128:(c_ + 1) * 128, t0:t0 + NT], dn.x.t[:, c_, :], [dn.x.r[c_]], [], is_out=True)
        P.emit("p")


def build_sample(nc, dr, cfg):
    NT = 128
    npg = cfg.npg
    RS_ROWS = 4
    nslab = 128 // RS_ROWS
    npast = npg * 128
    with ExitStack() as stack:
        P = Prog(nc)
        k = K(nc, P, stack, "s_")
        c = load_consts(k, dr, cfg, True)
        dn = Dense(k, NT, dr, c, wmode=("save" if SAVE_BF16 else "cast"), nslots=2)
        ident = k.sb("ident", [128, 128], F32)
        k.DMA("sp", ident.t[:], dr["ident"], [], ident.r)
        mnew = k.sb("mnew", [128, 128], F32)
        k.DMA("sp", mnew.t[:], dr["mnew"], [], mnew.r)
        own = KVCache(k, "own", 128)
        slab = [KVCache(k, "slab%d" % i, RS_ROWS * 128, with_kit=False) for i in range(2)]
        ncol = npast + 128
        scores = k.sb("scores", [128, ncol], F32)
        mslab = k.sb("mslab", [128, 2, RS_ROWS * 128], BF16, 2)
        junk = k.sb("junk", [128, 1024], BF16)
        bis = k.sb("bis", [128, 32], F32)
        idx = k.sb("idx", [128, 4], I32)
        for b in range(cfg.ndec):
            k.DMA("sp", idx.t[0:npg, b:b + 1], dr["page_table"][b, :].rearrange("(n o) -> n o", o=1), [], idx.r)
        idxs = k.sb("idxs", [128, cfg.ndec, nslab], I32)
        for sl in range(nslab):
            k.TS("dve", idxs.t[:, :, sl], idx.t[:, 0:cfg.ndec], float(nslab), float(sl), ALU.mult, ALU.add, idx.r, idxs.r)
        kig8 = k.sb("kig8", [128, 4, RS_ROWS * ID], F32, 4)
        kit8 = k.sb("kit8", [64, 8, RS_ROWS * 128], BF16, 8)
        dn.QITz = k.sb("QITz", [64, 4, IH, 128], BF16)
        k.MS("pool", dn.QITz.t[:], 0.0, dn.QITz.r)
        gkb = [k.sb("gkb%d" % i, [128, RS_ROWS, NKV * HD], BF16) for i in range(2)]
        kg = [k.sb("kg%d" % i, [128, RS_ROWS, NKV * HD], F32) for i in range(2)]
        vg = [k.sb("vg%d" % i, [128, RS_ROWS, NKV * HD], F32) for i in range(2)]
        OT = [dn.banks[4], dn.banks[5]]
        RS = [dn.banks[6], dn.banks[7]]
        for c_ in range(DC):
            k.DMA("sp", dn.x.t[:, c_, :], dr["xsT"][c_ * 128:(c_ + 1) * 128, :], [], [dn.x.r[c_]])
        gi = 0
        for L in range(cfg.depth):
            dn.ffn(c["g_ffn1"][L], dr["w_up1"][L], dr["w_down1"][L])

            def out_k(kh, ap, res, L=L):
                k.DMA("sp", dr["o_ksT"][L, kh * 128:(kh + 1) * 128, :], ap, [res], [], is_out=True)

            def out_v(tt, ap, res, L=L):
                k.DMA("sp", dr["o_vs"][L], ap, [res], [], is_out=True)

            def out_ki(ap, res, L=L):
                k.DMA("sp", dr["o_kisT"][L], ap, [res], [], is_out=True)

            def out_va(t, L=L):
                k.DMA("sp", dr["o_vas"][L], t.t[:], t.r, [], is_out=True)

            dn.mixer_dense(L, own, 0, out_k, out_v, out_ki, out_va=out_va, sample=True)
            assert RS_ROWS == 4 and npg == 128
            W_ = RS_ROWS * 128
            for sl in range(nslab):
                slots = []
                for b in range(cfg.ndec):
                    i8 = gi % 8
                    i4 = gi % 4
                    gi += 1
                    slots.append(i8)
                    src = dr["cache_kidx"][L].rearrange("p (s r) d -> (p s) (r d)", r=RS_ROWS)
                    k.P.dma("pool", lambda e, i4=i4, src=src, b=b, sl=sl: e.indirect_dma_start(
                        out=kig8.t[:, i4, :], out_offset=None, in_=src,
                        in_offset=bass.IndirectOffsetOnAxis(ap=idxs.t[:, b, sl:sl + 1], axis=0)), idxs.r, [kig8.r[i4]])
                    pt = dn.bank((0, 1))
                    for r in range(RS_ROWS):
                        k.TR(pt.t[0:64, r * 128:(r + 1) * 128], kig8.t[:, i4, r * ID:(r + 1) * ID], ident.t[:], [kig8.r[i4]] + ident.r, pt.r)
                    k.CP("act", kit8.t[:, i8, :], pt.t[0:64, 0:W_], pt.r, [kit8.r[i8]])
                for hh in range(IH):
                    pd = dn.bank((2, 3))
                    for b in range(cfg.ndec):
                        k.MM(pd.t[:, 0:W_], dn.QITz.t[:, b, hh, :], kit8.t[:, slots[b], :], b == 0, b == cfg.ndec - 1,
                             dn.QITz.r + [kit8.r[slots[b]]], pd.r)
                    ta, tr = dn.tmp()
                    ti = (dn.tmp_i - 1) % 3
                    ta = dn.tmpf.t[:, ti, :]
                    k.ACT(ta[:, 0:W_], pd.t[:, 0:W_], AF.Relu, pd.r, [tr])
                    sc = scores.t[:, sl * W_:(sl + 1) * W_]
                    if hh == 0:
                        k.TS("dve", sc, ta[:, 0:W_], dn.wtok.t[:, 0, 0:1], None, ALU.mult, None, [tr, dn.wtok.r[0]], scores.r)
                    else:
                        k.STT(sc, ta[:, 0:W_], dn.wtok.t[:, 0, hh:hh + 1], sc, ALU.mult, ALU.add, [tr, dn.wtok.r[0]] + scores.r, scores.r)
            dn.index_scores(own, 0, 128, scores, prange=(0, 128), col0=npast)
            k.TT("dve", scores.t[:, npast:ncol], mnew.t[:], scores.t[:, npast:ncol], ALU.add, scores.r + mnew.r, scores.r)
            dn.bisect_mask(scores, ncol, cfg.ksel_s, None, junk, bis)
            if DBG_STAGE == "sdbg" and L == 0:
                k.DMA("sp", dr["o_dbg_sc"], scores.t[:], scores.r, [], is_out=True)
                k.DMA("sp", dr["o_dbg_bis"], bis.t[:, 0:1], bis.r, [], is_out=True)
            for b in range(cfg.ndec):
                for sl in range(nslab):
                    r0 = sl * RS_ROWS
                    gk, gv = kg[gi % 2], vg[gi % 2]
                    sb_ = slab[gi % 2]
                    gi += 1
                    srck = dr["cache_k"][L].rearrange("p (s r) d -> (p s) (r d)", r=RS_ROWS)
                    srcv = dr["cache_v"][L].rearrange("p (s r) d -> (p s) (r d)", r=RS_ROWS)
                    k.P.dma("pool", lambda e, gk=gk, srck=srck, b=b, sl=sl: e.indirect_dma_start(
                        out=gk.t[:, :, :].rearrange("p r d -> p (r d)"), out_offset=None, in_=srck,
                        in_offset=bass.IndirectOffsetOnAxis(ap=idxs.t[:, b, sl:sl + 1], axis=0)), idxs.r, gk.r)
                    k.P.dma("pool", lambda e, gv=gv, srcv=srcv, b=b, sl=sl: e.indirect_dma_start(
                        out=gv.t[:, :, :].rearrange("p r d -> p (r d)"), out_offset=None, in_=srcv,
                        in_offset=bass.IndirectOffsetOnAxis(ap=idxs.t[:, b, sl:sl + 1], axis=0)), idxs.r, gv.r)
                    for r in range(RS_ROWS):
                        k.CP("dve", sb_.V.t[:, r, :, :].rearrange("p a b -> p (a b)"), gv.t[:, r, :], gv.r, [sb_.Vr[r]])
                    gb_ = gkb[gi % 2]
                    k.CP("dve", gb_.t[:].rearrange("p r d -> p (r d)"), gk.t[:].rearrange("p r d -> p (r d)"), gk.r, gb_.r)
                    pt = dn.bank((0, 1))
                    ptb = pt.t[:].bitcast(BF16)
                    for kh in range(NKV):
                        for r in range(RS_ROWS):
                            c0 = (kh * RS_ROWS + r) * 128
                            k.TR(ptb[:, c0:c0 + 128], gb_.t[:, r, kh * 128:(kh + 1) * 128], c["I4"].t[:, 0, :], gb_.r + c["I4"].r, pt.r)
                    for kh in range(NKV):
                        k.CP("act", sb_.KT.t[:, kh, 0:RS_ROWS * 128], ptb[:, kh * RS_ROWS * 128:(kh + 1) * RS_ROWS * 128], pt.r,
                             [sb_.KTr[r] for r in range(RS_ROWS)])
                    mi = gi % 2
                    mv = Tl(mslab.t[:, mi, :])
                    mv.r = [mslab.r[mi]]
                    W_ = RS_ROWS * 128
                    k.TS("dve", mv.t[:, 0:W_], scores.t[:, sl * W_:(sl + 1) * W_], bis.t[:, 0:1], MB, ALU.is_lt, ALU.mult, scores.r + bis.r, mv.r)
                    dn.attend(sb_, 0, RS_ROWS, mv, 0, 32, 32 * b, sl == 0, False, (OT, RS))
                mv = Tl(mslab.t[:, 0, :])
                mv.r = [mslab.r[0]]
                k.TS("dve", mv.t[:, 0:128], scores.t[:, npast:ncol], bis.t[:, 0:1], MB, ALU.is_lt, ALU.mult, scores.r + bis.r, mv.r)
                dn.attend(own, 0, 1, mv, 0, 32, 32 * b, False, True, (OT, RS))
                dn.attn_finish(0, 32, 32 * b, (OT, RS))
            dn.merge_out(L)
            dn.ffn(c["g_ffn2"][L], dr["w_up2"][L], dr["w_down2"][L])
        for c_ in range(DC):
            k.DMA("sp", dr["o_ysT"][c_ * 128:(c_ + 1) * 128, :], dn.x.t[:, c_, :], [dn.x.r[c_]], [], is_out=True)
        P.emit("s")


def build_nc(cfg, do_prompt=True, do_sample=True):
    nc = bass.Bass("TRN2", target_bir_lowering=False)
    Ld = cfg.depth
    S, ns = cfg.S, cfg.nseq

    def din(name, shape, dt=F32):
        return nc.dram_tensor(name, list(shape), dt, kind="ExternalInput").ap()

    def dout(name, shape):
        return nc.dram_tensor(name, list(shape), F32, kind="ExternalOutput").ap()

    dr = {}
    dr["xT"] = din("xT", [ns, D, S])
    dr["xsT"] = din("xsT", [D, 128])
    dr["cache_k"] = [din("cache_k%d" % L, [cfg.pool, 128, NKV * HD]) for L in range(Ld)]
    dr["cache_v"] = [din("cache_v%d" % L, [cfg.pool, 128, NKV * HD]) for L in range(Ld)]
    dr["cache_kidx"] = [din("cache_kidx%d" % L, [cfg.pool, 128, ID]) for L in range(Ld)]
    dr["page_table"] = din("page_table", [cfg.ndec, cfg.npg], I32)
    for nm in ("g_ffn1", "g_mix", "g_ffn2"):
        dr[nm] = din(nm, [Ld, 128, DC])
    dr["g_q"] = din("g_q", [Ld, 128, 1])
    dr["g_k"] = din("g_k", [Ld, 128, 1])
    dr["g_v"] = din("g_v", [Ld, 128, D])
    dr["b_s"] = din("b_s", [Ld, 1, 8, 128])
    dr["wsT"] = din("wsT", [Ld, 8, 128, 128])
    for nm, kk, nn in (("w_up1", D, 2 * DFF), ("w_down1", DFF, D), ("w_in", D, DIN), ("w_pa", D, D), ("w_pb", D, D),
                       ("w_out", D, D), ("w_up2", D, 2 * DFF), ("w_down2", DFF, D)):
        dr[nm] = din(nm, [Ld, kk, nn])
    for nm, kk, nn in (("w_up1", D, 2 * DFF), ("w_down1", DFF, D), ("w_in", D, DIN), ("w_pa", D, D), ("w_pb", D, D),
                       ("w_out", D, D), ("w_up2", D, 2 * DFF), ("w_down2", DFF, D)):
        dr[nm + "_bf"] = nc.dram_tensor(nm + "_bf", [Ld, kk, nn], BF16, kind="Internal").ap()
    dr["I4"] = din("I4", [128, 4, 128])
    dr["tri"] = din("tri", [128, 128])
    dr["triu"] = din("triu", [128, 128])
    dr["ident"] = din("ident", [128, 128])
    dr["mnew"] = din("mnew", [128, 128])
    dr["o_yT"] = dout("o_yT", [ns, D, S])
    dr["o_kT"] = dout("o_kT", [Ld, ns, NKV * HD, S])
    dr["o_v"] = dout("o_v", [Ld, ns, S, NKV * HD])
    dr["o_kiT"] = dout("o_kiT", [Ld, ns, ID, S])
    dr["o_ysT"] = dout("o_ysT", [D, 128])
    dr["o_ksT"] = dout("o_ksT", [Ld, NKV * HD, 128])
    dr["o_vs"] = dout("o_vs", [Ld, 128, NKV * HD])
    dr["o_kisT"] = dout("o_kisT", [Ld, ID, 128])
    dr["o_vas"] = dout("o_vas", [Ld, 128, D])
    if DBG_STAGE == "sdbg":
        dr["o_dbg_sc"] = dout("o_dbg_sc", [128, cfg.npg * 128 + 128])
        dr["o_dbg_bis"] = dout("o_dbg_bis", [128, 1])
    if do_sample:
        build_sample(nc, dr, cfg)
    if do_prompt:
        build_prompt(nc, dr, cfg)
    return nc


def host_consts():
    I4 = np.tile(np.eye(128, dtype=np.float32)[:, None, :], (1, 4, 1))
    t = np.arange(128)
    tri = np.where(t[None, :] <= t[:, None], 0.0, NEG).astype(np.float32)
    triu = (t[None, :] >= t[:, None]).astype(np.float32)
    ident = np.eye(128, dtype=np.float32)
    b_ = t // 32
    tl = t % 32
    ok = (b_[:, None] == b_[None, :]) & (tl[None, :] <= tl[:, None]) & (tl[None, :] < 4)
    mnew = np.where(ok, 0.0, NEG).astype(np.float32)
    return {"I4": I4, "tri": tri, "triu": triu, "ident": ident, "mnew": mnew}


def make_in_maps(inp, cfg, ncores):
    Ld = cfg.depth
    f = np.float32
    shared = host_consts()
    for nm in ("g_ffn1", "g_mix", "g_ffn2"):
        shared[nm] = np.ascontiguousarray(np.asarray(inp[nm], f).reshape(Ld, DC, 128).transpose(0, 2, 1))
    shared["g_q"] = np.asarray(inp["g_q"], f).reshape(Ld, 128, 1)
    shared["g_k"] = np.asarray(inp["g_k"], f).reshape(Ld, 128, 1)
    shared["g_v"] = np.ascontiguousarray(np.broadcast_to(np.asarray(inp["g_v"], f)[:, None, :], (Ld, 128, D)))
    shared["b_s"] = np.asarray(inp["b_s"], f).reshape(Ld, 1, 8, 128)
    shared["wsT"] = np.ascontiguousarray(np.asarray(inp["w_s"], f).transpose(0, 1, 3, 2))
    for nm in ("w_up1", "w_down1", "w_in", "w_pa", "w_pb", "w_out", "w_up2", "w_down2"):
        shared[nm] = np.asarray(inp[nm], f)
    pool = cfg.pool
    ck = np.asarray(inp["cache_k"], f).reshape(Ld, pool, 128, NKV * HD)
    cv = np.asarray(inp["cache_v"], f).reshape(Ld, pool, 128, NKV * HD)
    cki = np.asarray(inp["cache_kidx"], f)
    for L in range(Ld):
        shared["cache_k%d" % L] = ck[L]
        shared["cache_v%d" % L] = cv[L]
        shared["cache_kidx%d" % L] = cki[L]
    xp = np.asarray(inp["x_prompt"], f)
    xs = np.asarray(inp["x_sample"], f)
    pt = np.asarray(inp["page_table"], np.int32)
    maps = []
    for ci in range(ncores):
        m = dict(shared)
        m["xT"] = np.ascontiguousarray(xp[ci * cfg.nseq:(ci + 1) * cfg.nseq].transpose(0, 2, 1))
        xsT = np.zeros((D, 128), f)
        for b in range(cfg.ndec):
            xsT[:, 32 * b:32 * b + cfg.tdec] = xs[ci * cfg.ndec + b].T
        m["xsT"] = xsT
        m["page_table"] = np.ascontiguousarray(pt[ci * cfg.ndec:(ci + 1) * cfg.ndec])
        maps.append(m)
    return maps


def assemble(results, cfg, ncores):
    Ld = cfg.depth
    ns, S, nd, T = cfg.nseq, cfg.S, cfg.ndec, cfg.tdec
    B, DB = ns * ncores, nd * ncores
    y = np.empty((B, S, D), np.float32)
    nk = np.empty((Ld, B, S, NKV, HD), np.float32)
    nv = np.empty((Ld, B, S, NKV, HD), np.float32)
    nki = np.empty((Ld, B, S, ID), np.float32)
    ys = np.empty((DB, T, D), np.float32)
    ks = np.empty((Ld, DB, T, NKV, HD), np.float32)
    vs = np.empty((Ld, DB, T, NKV, HD), np.float32)
    kis = np.empty((Ld, DB, T, ID), np.float32)
    vas = np.empty((Ld, DB, T, D), np.float32)
    for ci, r in enumerate(results):
        sl = slice(ci * ns, (ci + 1) * ns)
        y[sl] = r["o_yT"].transpose(0, 2, 1)
        nk[:, sl] = r["o_kT"].transpose(0, 1, 3, 2).reshape(Ld, ns, S, NKV, HD)
        nv[:, sl] = r["o_v"].reshape(Ld, ns, S, NKV, HD)
        nki[:, sl] = r["o_kiT"].transpose(0, 1, 3, 2)
        for b in range(nd):
            gb = ci * nd + b
            cols = slice(32 * b, 32 * b + T)
            ys[gb] = r["o_ysT"][:, cols].T
            ks[:, gb] = r["o_ksT"][:, :, cols].transpose(0, 2, 1).reshape(Ld, T, NKV, HD)
            vs[:, gb] = r["o_vs"][:, cols, :].reshape(Ld, T, NKV, HD)
            kis[:, gb] = r["o_kisT"][:, :, cols].transpose(0, 2, 1)
            vas[:, gb] = r["o_vas"][:, cols, :]
    return (y, ys, nk, nv, nki, ks, vs, kis, vas)


def kernel(**inputs):
    cfg = FULL
    nc = build_nc(cfg)
    maps = make_in_maps(inputs, cfg, NCORES)
    res = run_bass_kernel_spmd(nc, maps, core_ids=list(range(NCORES)))
    return assemble(res.results, cfg, NCORES)
```

```python
from contextlib import ExitStack
import numpy as np
import concourse.bass as bass
import concourse.mybir as mybir
from concourse.bass_utils import run_bass_kernel_spmd

F32 = mybir.dt.float32
BF16 = mybir.dt.bfloat16
I32 = mybir.dt.int32
AF = mybir.ActivationFunctionType
ALU = mybir.AluOpType

import os as _os0
SAME_ENGINE_SYNC = _os0.environ.get('SES', '1') == '1'
EPOCH = 20000
NCORES = 8
SAVE_BF16 = True
import os as _os
DBG_STAGE = _os.environ.get('DBG_STAGE', '')


class Res:
    __slots__ = ("last_w", "readers", "dsem", "dcount")

    def __init__(self):
        self.last_w = None
        self.readers = []
        self.dsem = None
        self.dcount = 0


class Op:
    __slots__ = ("eng", "fn", "deps", "is_dma", "signal", "sem", "val", "key")

    def __init__(self, eng, fn, is_dma, key=None):
        self.eng = eng
        self.fn = fn
        self.deps = []
        self.is_dma = is_dma
        self.signal = False
        self.sem = None
        self.val = 0
        self.key = key


class Prog:
    ENGS = ("pe", "act", "dve", "pool", "sp")

    def __init__(self, nc):
        self.nc = nc
        self.ops = []
        self.sems = []
        self.out_dmas = []

    def new_sem(self, name):
        s = self.nc.alloc_semaphore(name=name)
        self.sems.append(s)
        return s

    def _add(self, op, reads, writes):
        deps = set()
        for r in reads:
            if r.last_w is not None:
                deps.add(r.last_w)
        for w in writes:
            if w.last_w is not None:
                deps.add(w.last_w)
            for rd in w.readers:
                deps.add(rd)
        deps.discard(op)
        op.deps = list(deps)
        for r in reads:
            r.readers.append(op)
        for w in writes:
            w.last_w = op
            w.readers = []
        self.ops.append(op)
        return op

    def op(self, eng, fn, reads=(), writes=()):
        return self._add(Op(eng, fn, False), list(reads), list(writes))

    def dma(self, eng, fn, reads=(), writes=(), key=None, is_out=False):
        reads = list(reads)
        writes = list(writes)
        if key is None:
            key = writes[0] if writes else reads[0]
        o = Op(eng, fn, True, key)
        self._add(o, reads, writes)
        o.signal = True
        if is_out:
            self.out_dmas.append(o)
        return o

    def emit(self, tag):
        nc = self.nc
        ops = self.ops

        def skip(p, o):
            return (not p.is_dma) and p.eng == o.eng and (not o.is_dma) and (
                p.eng == "pe" or not SAME_ENGINE_SYNC)

        for o in ops:
            for p in o.deps:
                if p.is_dma or skip(p, o):
                    continue
                p.signal = True
        eng_sems = {}
        eng_cnt = {}
        for o in ops:
            if o.is_dma:
                k = o.key
                if k.dsem is None:
                    k.dsem = self.new_sem("%sd%d" % (tag, len(self.sems)))
                k.dcount += 16
                o.sem = k.dsem
                o.val = k.dcount
            elif o.signal:
                c = eng_cnt.get(o.eng, 0)
                ep = c // EPOCH
                lst = eng_sems.setdefault(o.eng, [])
                if ep >= len(lst):
                    lst.append(self.new_sem("%se_%s_%d" % (tag, o.eng, ep)))
                o.sem = lst[ep]
                o.val = c - ep * EPOCH + 1
                eng_cnt[o.eng] = c + 1
        by_eng = {e: [] for e in self.ENGS}
        for o in ops:
            by_eng[o.eng].append(o)
        finals = {}
        for o in [o_ for o_ in ops if o_.is_dma]:
            k = id(o.sem)
            if k not in finals or finals[k][1] < o.val:
                finals[k] = (o.sem, o.val)

        def run(engine, lst, extra_final=None):
            waited = {}
            for o in lst:
                need = {}
                for p in o.deps:
                    if not p.signal or skip(p, o):
                        continue
                    k = id(p.sem)
                    if k not in need or need[k][1] < p.val:
                        need[k] = (p.sem, p.val)
                for k, (sem, val) in need.items():
                    if waited.get(k, 0) >= val:
                        continue
                    engine.wait_ge(sem, val)
                    waited[k] = val
                ins = o.fn(engine)
                if o.signal:
                    ins.then_inc(o.sem, 16 if o.is_dma else 1)
            if extra_final:
                for sem, val in extra_final:
                    if waited.get(id(sem), 0) < val:
                        engine.wait_ge(sem, val)

        with nc.Block() as block:
            @block.sync
            def _(e):
                run(e, by_eng["sp"], list(finals.values()))
            if by_eng["pe"]:
                @block.tensor
                def _(e):
                    run(e, by_eng["pe"])
            if by_eng["act"]:
                @block.scalar
                def _(e):
                    run(e, by_eng["act"])
            if by_eng["dve"]:
                @block.vector
                def _(e):
                    run(e, by_eng["dve"])
            if by_eng["pool"]:
                @block.gpsimd
                def _(e):
                    run(e, by_eng["pool"])
        nc.clear_and_free_semaphores(self.sems)
        nc.all_engine_barrier()


D = 1024
DC = 8
DFF = 2816
FC = 22
DIN = 6216
NH = 8
NKV = 2
HD = 128
IH = 8
ID = 64
C_U, C_V, C_Q, C_K, C_VV, C_QI, C_KI, C_WI, C_GA, C_GB = 0, 1024, 2048, 3072, 3328, 3584, 4096, 4160, 4168, 5192
EPS = 1e-6
NEG = -1.0e30
MB = -30000.0
BIS_ITERS = 20
BIS_LO = -16.0
BIS_W = 32.0


class Cfg:
    def __init__(self, nseq, S, ksel, ndec, tdec, npg, pool, depth=2, ksel_s=256):
        self.nseq, self.S, self.ksel, self.ksel_s = nseq, S, ksel, ksel_s
        self.ndec, self.tdec, self.npg, self.pool, self.depth = ndec, tdec, npg, pool, depth


FULL = Cfg(2, 2048, 256, 4, 4, 128, 5120)


class Tl:
    def __init__(self, t, nres=1):
        self.t = t
        self.r = [Res() for _ in range(nres)]


class K:
    def __init__(self, nc, P, stack, pfx=""):
        self.nc, self.P, self.stack, self.pfx = nc, P, stack, pfx
        self.wslot_i = 0

    def sb(self, name, shape, dt, nres=1):
        t = self.stack.enter_context(self.nc.sbuf_tensor(self.pfx + name, list(shape), dt))
        return Tl(t, nres)

    def ps(self, name, shape, dt=F32, nres=1):
        t = self.stack.enter_context(self.nc.psum_tensor(self.pfx + name, list(shape), dt))
        return Tl(t, nres)

    def MM(self, out, lhsT, rhs, start, stop, R, W):
        self.P.op("pe", lambda e: e.matmul(out, lhsT=lhsT, rhs=rhs, start=start, stop=stop), R, W)

    def TR(self, out, in_, ident, R, W):
        self.P.op("pe", lambda e: e.transpose(out=out, in_=in_, identity=ident), R, W)

    def ACT(self, out, in_, func, R, W, bias=None, scale=None):
        kw = {}
        if bias is not None:
            kw["bias"] = bias
        if scale is not None:
            kw["scale"] = scale
        self.P.op("act", lambda e: e.activation(out=out, in_=in_, func=func, **kw), R, W)

    def TS(self, eng, out, in0, s1, s2, op0, op1, R, W, accum_out=None):
        kw = {}
        if op1 is not None:
            kw["op1"] = op1
        if accum_out is not None:
            kw["accum_out"] = accum_out
        self.P.op(eng, lambda e: e.tensor_scalar(out=out, in0=in0, scalar1=s1, scalar2=s2, op0=op0, **kw), R, W)

    def TT(self, eng, out, in0, in1, op, R, W):
        self.P.op(eng, lambda e: e.tensor_tensor(out=out, in0=in0, in1=in1, op=op), R, W)

    def STT(self, out, in0, scalar, in1, op0, op1, R, W):
        self.P.op("dve", lambda e: e.scalar_tensor_tensor(out=out, in0=in0, scalar=scalar, in1=in1, op0=op0, op1=op1), R, W)

    def CP(self, eng, out, in_, R, W):
        if eng == "act":
            self.P.op("act", lambda e: e.copy(out=out, in_=in_), R, W)
        elif eng == "dve":
            self.P.op("dve", lambda e: e.tensor_scalar(out=out, in0=in_, scalar1=1.0, scalar2=None, op0=ALU.mult), R, W)
        else:
            self.P.op(eng, lambda e: e.tensor_copy(out=out, in_=in_), R, W)

    def MS(self, eng, ap, val, W):
        self.P.op(eng, lambda e: e.memset(ap, val), [], W)

    def DMA(self, eng, out, in_, R, W, is_out=False, key=None):
        self.P.dma(eng, lambda e: e.dma_start(out=out, in_=in_), R, W, key=key, is_out=is_out)


class Dense:
    def __init__(self, k, NT, dr, consts, wmode="cast", out_q="sp", nslots=3):
        self.k, self.NT, self.dr, self.c = k, NT, dr, consts
        self.wmode, self.out_q = wmode, out_q
        self.QITz = None
        NTT = NT // 128
        self.NTT = NTT
        k_ = k
        self.x = k_.sb("x", [128, DC, NT], F32, DC)
        self.h = k_.sb("h", [128, DC, NT], BF16, DC)
        self.sq = k_.sb("sq", [128, 2, NT], BF16, 2)
        self.rstd = k_.sb("rstd", [128, NT], F32)
        hidraw = k_.sb("hidraw", [128, FC * NT // 2], F32, FC)
        self.hidraw = hidraw
        self.hid = Tl(hidraw.t[:].bitcast(BF16).rearrange("p (c n) -> p c n", c=FC), 0)
        self.hid.r = hidraw.r
        self.tmpf = k_.sb("tmpf", [128, 3, 512], F32, 3)
        self.tmp_i = 0
        self.nslots = nslots
        self.wring = k_.sb("wring", [128, nslots, 4096], BF16, nslots)
        self.wsave = [Res() for _ in range(nslots)]
        self.banks = [k_.ps("bank%d" % i, [128, 512]) for i in range(8)]
        self.bank_i = 0
        self.eps = k_.sb("eps", [128, 1], F32)
        k_.MS("pool", self.eps.t[:], EPS, self.eps.r)
        self.QT = k_.sb("QT", [128, NH, NT], BF16, NH)
        self.QIT = k_.sb("QIT", [64, IH, NT], BF16, IH)
        self.aT = k_.sb("aT", [128, DC, NT], BF16, DC)
        self.attnT = k_.sb("attnT", [128, NH, NT], BF16, NH)
        self.mT = self.QT
        self.wtok = k_.sb("wtok", [128, NTT, IH], F32, NTT)
        self.vg = k_.sb("vg", [128, D], F32)
        self.vn = k_.sb("vn", [128, 1, D], BF16, 1)
        self.vnf = k_.sb("vnf", [128, D], F32) if NT == 128 else None
        self.vjunk = k_.sb("vjunk", [128, D], BF16)
        self.sm = k_.sb("sm", [128, 8], F32, 4)
        self.vstage = k_.sb("vstage", [128, 1, NKV * HD], F32, 1)
        self.kstage = k_.sb("kstage", [128, 1, NT], F32, 1)

    def bank(self, group=(0, 1, 2, 3)):
        b = self.banks[group[self.bank_i % len(group)]]
        self.bank_i += 1
        return b

    def tmp(self):
        i = self.tmp_i % 3
        self.tmp_i += 1
        return self.tmpf.t[:, i, 0:self.NT], self.tmpf.r[i]

    def wload(self, w2d, KC, c0, nc_):
        k = self.k
        i = k.wslot_i % self.nslots
        k.wslot_i += 1
        view = self.wring.t[:, i, 0:KC * nc_].rearrange("p (k n) -> p k n", k=KC)
        src = w2d.rearrange("(k p) n -> p k n", p=128)[:, :, c0:c0 + nc_]
        kk, nn = w2d.shape
        L = w2d.offset // (kk * nn)
        bf = self.dr[w2d.tensor.name + "_bf"][L].rearrange("(k p) n -> p k n", p=128)[:, :, c0:c0 + nc_]
        wr = self.wring.r[i]
        parts = [(0, KC)] if KC <= 8 else [(0, KC // 2), (KC // 2, KC)]
        for (a0, a1) in parts:
            if self.wmode == "bf16":
                k.DMA("sp", view[:, a0:a1, :], bf[:, a0:a1, :], [], [wr])
            else:
                k.DMA("pool", view[:, a0:a1, :], src[:, a0:a1, :], [], [wr])
        if self.wmode == "save":
            for (a0, a1) in parts:
                k.DMA("sp", bf[:, a0:a1, :], view[:, a0:a1, :], [wr], [], is_out=True, key=self.wsave[i])
        return view, self.wring.r[i]

    def rmsnorm(self, gT):
        k, NT = self.k, self.NT
        st = self.bank((4,))
        for c in range(DC):
            j = c % 2
            k.ACT(self.sq.t[:, j, :], self.x.t[:, c, :], AF.Square, [self.x.r[c]], [self.sq.r[j]])
            k.MM(st.t[:, 0:NT], self.c["ones"].t[:], self.sq.t[:, j, :], c == 0, c == DC - 1, [self.sq.r[j], self.c["ones"].r[0]], st.r)
        k.ACT(self.rstd.t[:], st.t[:, 0:NT], AF.Ln, st.r + self.eps.r, self.rstd.r, bias=self.eps.t[:], scale=1.0 / D)
        k.ACT(self.rstd.t[:], self.rstd.t[:], AF.Exp, self.rstd.r, self.rstd.r, scale=-0.5)
        for c in range(DC):
            k.STT(self.h.t[:, c, :], self.x.t[:, c, :], gT.t[:, c:c + 1], self.rstd.t[:], ALU.mult, ALU.mult,
                  [self.x.r[c], gT.r[0], self.rstd.r[0]], [self.h.r[c]])

    def proj(self, wview, wres, col, M, src, KC, ps_ap, ps_res):
        k = self.k
        for kc in range(KC):
            k.MM(ps_ap, wview[:, kc, col:col + M], src.t[:, kc, :], kc == 0, kc == KC - 1,
                 [wres, src.r[kc]], ps_res)

    def ffn(self, gT, w_up, w_down):
        k, NT = self.k, self.NT
        self.rmsnorm(gT)
        for j0 in range(0, FC, 4):
            n = min(4, FC - j0)
            gv, gr = self.wload(w_up, DC, j0 * 128, n * 128)
            uv, ur = self.wload(w_up, DC, DFF + j0 * 128, n * 128)
            for i in range(n):
                pg = self.bank()
                self.proj(gv, gr, i * 128, 128, self.h, DC, pg.t[:, 0:NT], pg.r)
                pu = self.bank()
                self.proj(uv, ur, i * 128, 128, self.h, DC, pu.t[:, 0:NT], pu.r)
                ta, tr = self.tmp()
                k.ACT(ta, pg.t[:, 0:NT], AF.Silu, pg.r, [tr])
                k.TT("dve", self.hid.t[:, j0 + i, :], pu.t[:, 0:NT], ta, ALU.mult, pu.r + [tr], [self.hid.r[j0 + i]])
        if DBG_STAGE == "ffnup":
            return
        for c in range(DC):
            dv, drr = self.wload(w_down, FC, c * 128, 128)
            if True:
                pd = self.bank()
                self.proj(dv, drr, 0, 128, self.hid, FC, pd.t[:, 0:NT], pd.r)
                k.STT(self.x.t[:, c, :], pd.t[:, 0:NT], 0.5, self.x.t[:, c, :], ALU.mult, ALU.add,
                      pd.r + [self.x.r[c]], [self.x.r[c]])

    def headnorm(self, ps, g, out_ap, out_res, stage=None):
        k, NT = self.k, self.NT
        k.ACT(self.sq.t[:, 0, :], ps.t[:, 0:NT], AF.Square, ps.r, [self.sq.r[0]])
        st = self.bank((4,))
        k.MM(st.t[:, 0:NT], self.c["ones"].t[:], self.sq.t[:, 0, :], True, True, [self.sq.r[0], self.c["ones"].r[0]], st.r)
        rs, rr = self.tmp()
        k.ACT(rs, st.t[:, 0:NT], AF.Ln, st.r + self.eps.r, [rr], bias=self.eps.t[:], scale=1.0 / HD)
        k.ACT(rs, rs, AF.Exp, [rr], [rr], scale=-0.5)
        k.STT(out_ap, ps.t[:, 0:NT], g.t[:, 0:1], rs, ALU.mult, ALU.mult, ps.r + [g.r[0], rr], out_res)
        if stage is not None:
            sap, sres = stage
            k.STT(sap, ps.t[:, 0:NT], g.t[:, 0:1], rs, ALU.mult, ALU.mult, ps.r + [g.r[0], rr], sres)

    def mixer_dense(self, L, kv, tok0, out_k, out_v, out_ki, out_va=None, sample=False):
        k, NT, NTT, c, dr = self.k, self.NT, self.NTT, self.c, self.dr
        w_in = dr["w_in"][L]
        self.rmsnorm(c["g_mix"][L])
        k.DMA("sp", c["gv"][L].t[:], dr["g_v"][L], [], c["gv"][L].r)
        kt0 = tok0 // 128
        wv0, wr0 = self.wload(w_in, DC, C_V, 512)
        wv1, wr1 = self.wload(w_in, DC, C_V + 512, 512)
        for tt in range(NTT):
            for half, (wv, wr) in enumerate(((wv0, wr0), (wv1, wr1))):
                pv = self.bank()
                for kc in range(DC):
                    k.MM(pv.t[:, :], self.h.t[:, kc, tt * 128:(tt + 1) * 128], wv[:, kc, :], kc == 0, kc == DC - 1,
                         [self.h.r[kc], wr], pv.r)
                k.ACT(self.vg.t[:, half * 512:(half + 1) * 512], pv.t[:, :], AF.Gelu_apprx_tanh, pv.r, self.vg.r)
            ss = self.sm.t[:, 0:1]
            k.P.op("act", lambda e, ss=ss: e.activation(out=self.vjunk.t[:], in_=self.vg.t[:], func=AF.Square, accum_out=ss),
                   self.vg.r, [self.vjunk.r[0], self.sm.r[0]])
            k.ACT(self.sm.t[:, 1:2], ss, AF.Ln, [self.sm.r[0]] + self.eps.r, [self.sm.r[1]], bias=self.eps.t[:], scale=1.0 / D)
            k.ACT(self.sm.t[:, 1:2], self.sm.t[:, 1:2], AF.Exp, [self.sm.r[1]], [self.sm.r[1]], scale=-0.5)
            j = 0
            k.STT(self.vn.t[:, j, :], self.vg.t[:], self.sm.t[:, 1:2], c["gv"][L].t[:], ALU.mult, ALU.mult,
                  self.vg.r + [self.sm.r[1], c["gv"][L].r[0]], [self.vn.r[j]])
            if out_va is not None:
                k.STT(self.vnf.t[:], self.vg.t[:], self.sm.t[:, 1:2], c["gv"][L].t[:], ALU.mult, ALU.mult,
                      self.vg.r + [self.sm.r[1], c["gv"][L].r[0]], self.vnf.r)
                out_va(self.vnf)
            WT = c["WTs"][L] if sample else c["WT"][L]
            for g in range(8):
                psv = self.bank()
                k.MM(psv.t[:, 0:128], self.vn.t[:, j, g * 128:(g + 1) * 128], WT.t[:, g, :], True, False,
                     [self.vn.r[j], WT.r[0]], psv.r)
                k.MM(psv.t[:, 0:128], c["ones1"].t[0:1, :], c["bhi"][L].t[0:1, g, :], False, False,
                     [c["ones1"].r[0], c["bhi"][L].r[0]], psv.r)
                k.MM(psv.t[:, 0:128], c["ones1"].t[0:1, :], c["blo"][L].t[0:1, g, :], False, True,
                     [c["ones1"].r[0], c["blo"][L].r[0]], psv.r)
                k.CP("act", self.attnT.t[:, g, tt * 128:(tt + 1) * 128], psv.t[:, 0:128], psv.r, [self.attnT.r[g]])
        if DBG_STAGE == "md1":
            return
        for hb in range(2):
            wv, wr = self.wload(w_in, DC, C_U + hb * 512, 512)
            for i in range(4):
                g = hb * 4 + i
                pu = self.bank()
                self.proj(wv, wr, i * 128, 128, self.h, DC, pu.t[:, 0:NT], pu.r)
                ta, tr = self.sq.t[:, g % 2, :], self.sq.r[g % 2]
                k.ACT(ta, pu.t[:, 0:NT], AF.Gelu_apprx_tanh, pu.r, [tr])
                if DBG_STAGE != "md2a":
                    k.TT("dve", self.aT.t[:, g, :], self.attnT.t[:, g, :], ta, ALU.mult, [self.attnT.r[g], tr], [self.aT.r[g]])
        if DBG_STAGE in ("md2", "md2a"):
            return
        for hb in range(2):
            wv, wr = self.wload(w_in, DC, C_Q + hb * 512, 512)
            for i in range(4):
                hh = hb * 4 + i
                pq = self.bank()
                self.proj(wv, wr, i * 128, 128, self.h, DC, pq.t[:, 0:NT], pq.r)
                self.headnorm(pq, c["gq"][L], self.QT.t[:, hh, :], [self.QT.r[hh]])
        if DBG_STAGE == "md3":
            return
        wv, wr = self.wload(w_in, DC, C_K, 512)
        for kh in range(NKV):
            pk = self.bank()
            self.proj(wv, wr, kh * 128, 128, self.h, DC, pk.t[:, 0:NT], pk.r)
            j = 0
            kres = [kv.KTr[kt0 + t] for t in range(NTT)]
            self.headnorm(pk, c["gk"][L], kv.KT.t[:, kh, tok0:tok0 + NT], kres,
                          stage=(self.kstage.t[:, j, :], [self.kstage.r[j]]))
            out_k(kh, self.kstage.t[:, j, :], self.kstage.r[j])
        for tt in range(NTT):
            pv = self.bank()
            for kc in range(DC):
                k.MM(pv.t[:, 0:256], self.h.t[:, kc, tt * 128:(tt + 1) * 128], wv[:, kc, 256:512], kc == 0, kc == DC - 1,
                     [self.h.r[kc], wr], pv.r)
            j = 0
            k.CP("act", self.vstage.t[:, j, :], pv.t[:, 0:256], pv.r, [self.vstage.r[j]])
            k.CP("pool", kv.V.t[:, kt0 + tt, :, :].rearrange("p a b -> p (a b)"), self.vstage.t[:, j, :], [self.vstage.r[j]], [kv.Vr[kt0 + tt]])
            out_v(tt, self.vstage.t[:, j, :], self.vstage.r[j])
        if DBG_STAGE == "md4":
            return
        wv, wr = self.wload(w_in, DC, C_QI, 512)
        for hh in range(IH):
            pq = self.bank()
            self.proj(wv, wr, hh * 64, 64, self.h, DC, pq.t[0:64, 0:NT], pq.r)
            k.ACT(self.QIT.t[:, hh, :], pq.t[0:64, 0:NT], AF.Copy, pq.r, [self.QIT.r[hh]], scale=ID ** -0.5)
            if self.QITz is not None:
                for b_ in range(4):
                    k.ACT(self.QITz.t[:, b_, hh, 32 * b_:32 * b_ + 32], pq.t[0:64, 32 * b_:32 * b_ + 32], AF.Copy, pq.r, self.QITz.r,
                          scale=ID ** -0.5)
        wv, wr = self.wload(w_in, DC, C_KI, 72)
        pk = self.bank()
        self.proj(wv, wr, 0, 64, self.h, DC, pk.t[0:64, 0:NT], pk.r)
        k.CP("act", self.kstage.t[0:64, 0, :], pk.t[0:64, 0:NT], pk.r, [self.kstage.r[0]])
        k.CP("pool", kv.KIT.t[:, tok0:tok0 + NT], self.kstage.t[0:64, 0, :], [self.kstage.r[0]], [kv.KIr[kt0 + t] for t in range(NTT)])
        out_ki(self.kstage.t[0:64, 0, :], self.kstage.r[0])
        for tt in range(NTT):
            pw = self.bank()
            for kc in range(DC):
                k.MM(pw.t[:, 0:8], self.h.t[:, kc, tt * 128:(tt + 1) * 128], wv[:, kc, 64:72], kc == 0, kc == DC - 1,
                     [self.h.r[kc], wr], pw.r)
            k.ACT(self.wtok.t[:, tt, :], pw.t[:, 0:8], AF.Copy, pw.r, [self.wtok.r[tt]], scale=IH ** -0.5)

    def index_scores(self, kv, tt, nk, scores, prange=(0, 128), col0=0):
        k = self.k
        p0, p1 = prange
        for ch in range((nk + 511) // 512):
            w = min(512, nk - ch * 512)
            kts = [kv.KIr[t] for t in range(ch * 4, ch * 4 + (w + 127) // 128)]
            for hh in range(IH):
                pd = self.bank((0, 1))
                k.MM(pd.t[:, 0:w], self.QIT.t[:, hh, tt * 128:(tt + 1) * 128], kv.KIT.t[:, ch * 512:ch * 512 + w], True, True,
                     [self.QIT.r[hh]] + kts, pd.r)
                ti = self.tmp_i % 3
                self.tmp_i += 1
                ta, tr = self.tmpf.t[:, ti, :], self.tmpf.r[ti]
                k.ACT(ta[p0:p1, 0:w], pd.t[p0:p1, 0:w], AF.Relu, pd.r, [tr])
                sc = scores.t[p0:p1, col0 + ch * 512:col0 + ch * 512 + w]
                if hh == 0:
                    k.TS("dve", sc, ta[p0:p1, 0:w], self.wtok.t[p0:p1, tt, 0:1], None, ALU.mult, None,
                         [tr, self.wtok.r[tt]], scores.r)
                else:
                    k.STT(sc, ta[p0:p1, 0:w], self.wtok.t[p0:p1, tt, hh:hh + 1], sc, ALU.mult, ALU.add,
                          [tr, self.wtok.r[tt]] + scores.r, scores.r)

    def index_scores_pe(self, kv, tt, nk, scores, diag):
        k = self.k
        for hh in range(IH):
            k.TS("dve", diag.t[:, hh, :], self.c["I4"].t[:, 0, :], self.wtok.t[:, tt, hh:hh + 1], None, ALU.mult, None,
                 [self.c["I4"].r[0], self.wtok.r[tt]], diag.r)
        for ch in range((nk + 511) // 512):
            w = min(512, nk - ch * 512)
            kts = [kv.KIr[t] for t in range(ch * 4, ch * 4 + (w + 127) // 128)]
            psc = self.banks[3]
            pend = None
            for hh in range(IH):
                pd = self.banks[hh % 2]
                k.MM(pd.t[:, 0:w], self.QIT.t[:, hh, tt * 128:(tt + 1) * 128], kv.KIT.t[:, ch * 512:ch * 512 + w], True, True,
                     [self.QIT.r[hh]] + kts, pd.r)
                j = hh % 2
                k.ACT(self.sq.t[:, j, 0:w], pd.t[:, 0:w], AF.Relu, pd.r, [self.sq.r[j]])
                if pend is not None:
                    ph, pj = pend
                    k.MM(psc.t[:, 0:w], diag.t[:, ph, :], self.sq.t[:, pj, 0:w], ph == 0, False, diag.r + [self.sq.r[pj]], psc.r)
                pend = (hh, j)
            ph, pj = pend
            k.MM(psc.t[:, 0:w], diag.t[:, ph, :], self.sq.t[:, pj, 0:w], False, True, diag.r + [self.sq.r[pj]], psc.r)
            k.CP("act", scores.t[:, ch * 512:ch * 512 + w], psc.t[:, 0:w], psc.r, scores.r)

    def bisect_mask(self, scores, ncol, ksel, maskb, junk, bis):
        k = self.k
        lo, mid, g = bis.t[:, 0:1], bis.t[:, 1:2], bis.t[:, 3:4]
        cnt = bis.t[:, 2:3]
        cnt8 = bis.t[:, 8:32]
        JW = junk.t.shape[1]
        nchunk = (ncol + JW - 1) // JW
        W = BIS_W
        k.MS("dve", mid, BIS_LO + 0.5 * W, bis.r)
        for it in range(BIS_ITERS):
            for ci in range(nchunk):
                w = min(JW, ncol - ci * JW)
                k.TS("dve", junk.t[:, 0:w], scores.t[:, ci * JW:ci * JW + w], mid, 0.0, ALU.is_ge, ALU.add,
                     scores.r + bis.r, junk.r + bis.r, accum_out=cnt8[:, ci:ci + 1])
            if nchunk > 1:
                k.P.op("dve", lambda e: e.reduce_sum(out=cnt, in_=cnt8[:, 0:nchunk], axis=mybir.AxisListType.X), bis.r, bis.r)
                cc = cnt
            else:
                cc = cnt8[:, 0:1]
            if it < BIS_ITERS - 1:
                k.TS("dve", g, cc, ksel - 0.5, 0.5 * W, ALU.is_ge, ALU.mult, bis.r, bis.r)
                k.STT(mid, g, -0.25 * W, mid, ALU.add, ALU.add, bis.r, bis.r)
            else:
                k.TS("dve", g, cc, ksel - 0.5, 0.5 * W, ALU.is_ge, ALU.mult, bis.r, bis.r)
                k.STT(lo, g, -0.5 * W, mid, ALU.add, ALU.add, bis.r, bis.r)
            W *= 0.5
        if maskb is not None:
            k.TS("dve", maskb.t[:, 0:ncol], scores.t[:, 0:ncol], lo, MB, ALU.is_lt, ALU.mult, scores.r + bis.r, maskb.r)

    def attend(self, kv, tt, ntile, maskb, mcol0, nslot, slot0, first, last, acc):
        k = self.k
        OT, RS = acc
        N = 4 * nslot
        q0 = tt * 128 + slot0
        pend = None

        def flush(p):
            kh, kt, pT, pr, st, sp = p
            k.MM(OT[kh].t[:, 0:N], kv.V.t[:, kt, kh, :], pT, st, sp, [kv.Vr[kt], pr], OT[kh].r)
            k.MM(RS[kh].t[:, 0:N], self.c["ones"].t[:], pT, st, sp, [self.c["ones"].r[0], pr], RS[kh].r)

        for kh in range(NKV):
            for kt in range(ntile):
                pS = self.bank((2, 3))
                outv = pS.t[:, 0:N].rearrange("p (g t) -> p g t", g=4)
                k.MM(outv, kv.KT.t[:, kh, kt * 128:(kt + 1) * 128], self.QT.t[:, kh * 4:(kh + 1) * 4, q0:q0 + nslot], True, False,
                     [kv.KTr[kt]] + self.QT.r[kh * 4:(kh + 1) * 4], pS.r)
                k.MM(outv, maskb.t[:, mcol0 + kt * 128:mcol0 + (kt + 1) * 128], self.c["I4"].t[:, :, slot0:slot0 + nslot], False, True,
                     maskb.r + [self.c["I4"].r[0]], pS.r)
                i = self.tmp_i % 3
                self.tmp_i += 1
                pT = self.c["pT"].t[:, i, 0:N]
                pr = self.c["pT"].r[i]
                k.ACT(pT, pS.t[:, 0:N], AF.Exp, pS.r, [pr], scale=HD ** -0.5)
                if pend is not None:
                    flush(pend)
                pend = (kh, kt, pT, pr, first and kt == 0, last and kt == ntile - 1)
        if pend is not None:
            flush(pend)

    def attn_finish(self, tt, nslot, slot0, acc):
        k = self.k
        OT, RS = acc
        N = 4 * nslot
        for kh in range(NKV):
            ta, tr = self.tmp()
            k.ACT(ta[:, 0:N], RS[kh].t[:, 0:N], AF.Ln, RS[kh].r, [tr])
            k.ACT(ta[:, 0:N], ta[:, 0:N], AF.Exp, [tr], [tr], scale=-1.0)
            q0 = tt * 128 + slot0
            k.TT("dve", self.attnT.t[:, kh * 4:(kh + 1) * 4, q0:q0 + nslot], OT[kh].t[:, 0:N].rearrange("p (g t) -> p g t", g=4),
                 ta[:, 0:N].rearrange("p (g t) -> p g t", g=4), ALU.mult, OT[kh].r + [tr], self.attnT.r[kh * 4:(kh + 1) * 4])

    def merge_out(self, L):
        k, NT, dr = self.k, self.NT, self.dr
        w_in = dr["w_in"][L]
        for hb in range(2):
            wa, ra = self.wload(dr["w_pa"][L], DC, hb * 512, 512)
            wg, rg = self.wload(w_in, DC, C_GA + hb * 512, 512)
            for i in range(4):
                c = hb * 4 + i
                pa = self.bank()
                self.proj(wa, ra, i * 128, 128, self.aT, DC, pa.t[:, 0:NT], pa.r)
                pg = self.bank()
                self.proj(wg, rg, i * 128, 128, self.h, DC, pg.t[:, 0:NT], pg.r)
                ta, tr = self.tmp()
                k.ACT(ta, pg.t[:, 0:NT], AF.Sigmoid, pg.r, [tr])
                k.TT("dve", self.hid.t[:, c, :], pa.t[:, 0:NT], ta, ALU.mult, pa.r + [tr], [self.hid.r[c]])
        for hb in range(2):
            wb, rb = self.wload(dr["w_pb"][L], DC, hb * 512, 512)
            wg, rg = self.wload(w_in, DC, C_GB + hb * 512, 512)
            for i in range(4):
                c = hb * 4 + i
                pb = self.bank()
                self.proj(wb, rb, i * 128, 128, self.attnT, DC, pb.t[:, 0:NT], pb.r)
                pg = self.bank()
                self.proj(wg, rg, i * 128, 128, self.h, DC, pg.t[:, 0:NT], pg.r)
                ta, tr = self.tmp()
                k.ACT(ta, pg.t[:, 0:NT], AF.Sigmoid, pg.r, [tr])
                tb, trb = self.sq.t[:, c % 2, :], self.sq.r[c % 2]
                k.TT("dve", tb, pb.t[:, 0:NT], ta, ALU.mult, pb.r + [tr], [trb])
                k.TT("dve", self.mT.t[:, c, :], tb, self.hid.t[:, c, :], ALU.add, [trb, self.hid.r[c]], [self.mT.r[c]])
        for hb in range(2):
            wo, ro = self.wload(dr["w_out"][L], DC, hb * 512, 512)
            for i in range(4):
                c = hb * 4 + i
                po = self.bank()
                self.proj(wo, ro, i * 128, 128, self.mT, DC, po.t[:, 0:NT], po.r)
                k.TT("dve", self.x.t[:, c, :], po.t[:, 0:NT], self.x.t[:, c, :], ALU.add, po.r + [self.x.r[c]], [self.x.r[c]])


class KVCache:
    def __init__(self, k, name, nkeys, with_kit=True):
        nt = nkeys // 128
        self.KT = k.sb(name + "KT", [128, NKV, nkeys], BF16)
        self.V = k.sb(name + "V", [128, nt, NKV, HD], BF16)
        self.KIT = k.sb(name + "KIT", [64, nkeys], BF16) if with_kit else None
        self.KTr = [Res() for _ in range(nt)]
        self.Vr = [Res() for _ in range(nt)]
        self.KIr = [Res() for _ in range(nt)]


def load_consts(k, dr, cfg, sample):
    c = {}
    Lr = range(cfg.depth)
    c["ones"] = k.sb("ones", [128, 128], BF16)
    k.MS("pool", c["ones"].t[:], 1.0, c["ones"].r)
    c["ones1"] = k.sb("ones1", [1, 128], BF16)
    k.MS("pool", c["ones1"].t[:], 1.0, c["ones1"].r)
    c["I4"] = k.sb("I4", [128, 4, 128], BF16)
    k.DMA("pool", c["I4"].t[:], dr["I4"], [], c["I4"].r)
    c["tri"] = k.sb("tri", [128, 128], F32)
    k.DMA("sp", c["tri"].t[:], dr["tri"], [], c["tri"].r)
    triu = k.sb("triu", [128, 128], BF16)
    k.DMA("pool", triu.t[:], dr["triu"], [], triu.r)
    c["pT"] = k.sb("pT", [128, 3, 512], BF16, 3)
    for nm in ("g_ffn1", "g_mix", "g_ffn2"):
        c[nm] = []
        for L in Lr:
            t = k.sb("%s%d" % (nm, L), [128, DC], F32)
            k.DMA("sp", t.t[:], dr[nm][L], [], t.r)
            c[nm].append(t)
    for nm, src in (("gq", "g_q"), ("gk", "g_k")):
        c[nm] = []
        for L in Lr:
            t = k.sb("%s%d" % (nm, L), [128, 1], F32)
            k.DMA("sp", t.t[:], dr[src][L], [], t.r)
            c[nm].append(t)
    gvt = k.sb("gvt", [128, D], F32)
    c["gv"] = [gvt for L in Lr]
    c["bhi"], c["blo"], c["WT"], c["WTs"] = [], [], [], []
    bf = k.sb("bf", [1, 8, 128], F32)
    bh32 = k.sb("bh32", [1, 8, 128], F32)
    for L in Lr:
        if sample:
            for b in range(4):
                k.DMA("sp", bf.t[0:1, :, 32 * b:32 * b + 32], dr["b_s"][L][:, :, 0:32], [], bf.r)
        else:
            k.DMA("sp", bf.t[:], dr["b_s"][L], [], bf.r)
        bhi = k.sb("bhi%d" % L, [1, 8, 128], BF16)
        blo = k.sb("blo%d" % L, [1, 8, 128], BF16)
        k.CP("dve", bhi.t[:], bf.t[:], bf.r, bhi.r)
        k.CP("dve", bh32.t[:], bhi.t[:], bhi.r, bh32.r)
        k.TT("dve", blo.t[:], bf.t[:], bh32.t[:], ALU.subtract, bf.r + bh32.r, blo.r)
        c["bhi"].append(bhi)
        c["blo"].append(blo)
        WT = k.sb("WT%d" % L, [128, 8, 128], BF16)
        if sample:
            k.MS("pool", WT.t[:], 0.0, WT.r)
            for b in range(4):
                k.DMA("pool", WT.t[32 * b:32 * b + 4, :, 32 * b:32 * b + 4],
                      dr["wsT"][L][:, 0:4, 0:4].rearrange("g s t -> s g t"), [], WT.r)
        else:
            k.DMA("pool", WT.t[:], dr["wsT"][L].rearrange("g s t -> s g t"), [], WT.r)
        for g in range(8):
            k.TT("dve", WT.t[:, g, :], triu.t[:], WT.t[:, g, :], ALU.mult, WT.r + triu.r, WT.r)
        c["WTs" if sample else "WT"].append(WT)
    return c


def build_prompt(nc, dr, cfg):
    NT = 512
    with ExitStack() as stack:
        P = Prog(nc)
        k = K(nc, P, stack, "p_")
        c = load_consts(k, dr, cfg, False)
        dn = Dense(k, NT, dr, c, wmode=("bf16" if SAVE_BF16 else "cast"), out_q=("pool" if SAVE_BF16 else "sp"))
        OQ = dn.out_q
        S = cfg.S
        caches = [KVCache(k, "c%d" % L, S) for L in range(cfg.depth)]
        scores = [k.sb("scores%d" % i, [128, S], F32) for i in range(1)]
        sc2 = Tl(dn.hidraw.t[:, 2048:2048 + S], 0)
        sc2.r = dn.hidraw.r[8:16]
        scores.append(sc2)
        maskb = [k.sb("maskb%d" % i, [128, S], BF16) for i in range(1)]
        junk = Tl(dn.hidraw.t[:, 4096:4096 + S // 2].bitcast(BF16), 0)
        junk.r = dn.hidraw.r[16:20]
        diag = k.sb("diag", [128, IH, 128], BF16)
        bis = k.sb("bis", [128, 32], F32)
        OT = [dn.banks[4], dn.banks[5]]
        RS = [dn.banks[6], dn.banks[7]]
        qn = 0
        for s in range(cfg.nseq):
            for blk in range(S // NT):
                t0 = blk * NT
                for c_ in range(DC):
                    k.DMA(OQ, dn.x.t[:, c_, :], dr["xT"][s, c_ * 128:(c_ + 1) * 128, t0:t0 + NT], [], [dn.x.r[c_]])
                for L in range(cfg.depth):
                    kv = caches[L]
                    if DBG_STAGE == "none":
                        continue
                    if DBG_STAGE == "norm":
                        dn.rmsnorm(c["g_ffn1"][L])
                        continue
                    dn.ffn(c["g_ffn1"][L], dr["w_up1"][L], dr["w_down1"][L])
                    if DBG_STAGE in ("ffn", "ffnup"):
                        continue

                    def out_k(kh, ap, res, L=L, s=s, t0=t0):
                        k.DMA(OQ, dr["o_kT"][L, s, kh * 128:(kh + 1) * 128, t0:t0 + NT], ap, [res], [], is_out=True)

                    def out_v(tt, ap, res, L=L, s=s, t0=t0):
                        k.DMA(OQ, dr["o_v"][L, s, t0 + tt * 128:t0 + (tt + 1) * 128, :], ap, [res], [], is_out=True)

                    def out_ki(ap, res, L=L, s=s, t0=t0):
                        k.DMA(OQ, dr["o_kiT"][L, s, :, t0:t0 + NT], ap, [res], [], is_out=True)

                    dn.mixer_dense(L, kv, t0, out_k, out_v, out_ki)
                    if DBG_STAGE in ("dense", "md1", "md2", "md2a", "md3", "md4"):
                        continue
                    def idx(tt, kv=kv, blk=blk):
                        qg = blk * 4 + tt
                        nk = (qg + 1) * 128
                        sc = scores[tt % 2]
                        dn.index_scores_pe(kv, tt, nk, sc, diag)
                        k.TT("dve", sc.t[:, nk - 128:nk], c["tri"].t[:], sc.t[:, nk - 128:nk], ALU.add, sc.r + c["tri"].r, sc.r)

                    def bisect(tt, blk=blk):
                        qg = blk * 4 + tt
                        nk = (qg + 1) * 128
                        if nk > cfg.ksel:
                            dn.bisect_mask(scores[tt % 2], nk, cfg.ksel, None, junk, bis)

                    def mask(tt, blk=blk):
                        qg = blk * 4 + tt
                        nk = (qg + 1) * 128
                        sc, mb = scores[tt % 2], maskb[0]
                        if nk > cfg.ksel:
                            k.TS("dve", mb.t[:, 0:nk], sc.t[:, 0:nk], bis.t[:, 0:1], MB, ALU.is_lt, ALU.mult, sc.r + bis.r, mb.r)
                        else:
                            k.TS("dve", mb.t[:, 0:nk], sc.t[:, 0:nk], -1.0e29, MB, ALU.is_lt, ALU.mult, sc.r, mb.r)

                    def att(tt, kv=kv, blk=blk):
                        qg = blk * 4 + tt
                        dn.attend(kv, tt, qg + 1, maskb[0], 0, 128, 0, True, True, (OT, RS))
                        dn.attn_finish(tt, 128, 0, (OT, RS))

                    nq = NT // 128
                    idx(0)
                    bisect(0)
                    mask(0)
                    for tt in range(nq):
                        if tt + 1 < nq:
                            idx(tt + 1)
                            bisect(tt + 1)
                        att(tt)
                        if tt + 1 < nq:
                            mask(tt + 1)
                    dn.merge_out(L)
                    dn.ffn(c["g_ffn2"][L], dr["w_up2"][L], dr["w_down2"][L])
                for c_ in range(DC):
                    k.DMA(OQ, dr["o_yT"][s, c_ * 128:(c_ + 1) * 128, t0:t0 + NT], dn.x.t[:, c_, :], [dn.x.r[c_]], [], is_out=True)
        P.emit("p")


def build_sample(nc, dr, cfg):
    NT = 128
    npg = cfg.npg
    RS_ROWS = 4
    nslab = 128 // RS_ROWS
    npast = npg * 128
    with ExitStack() as stack:
        P = Prog(nc)
        k = K(nc, P, stack, "s_")
        c = load_consts(k, dr, cfg, True)
        dn = Dense(k, NT, dr, c, wmode=("save" if SAVE_BF16 else "cast"), nslots=2)
        ident = k.sb("ident", [128, 128], F32)
        k.DMA("sp", ident.t[:], dr["ident"], [], ident.r)
        mnew = k.sb("mnew", [128, 128], F32)
        k.DMA("sp", mnew.t[:], dr["mnew"], [], mnew.r)
        own = KVCache(k, "own", 128)
        slab = [KVCache(k, "slab%d" % i, RS_ROWS * 128, with_kit=False) for i in range(2)]
        ncol = npast + 128
        scores = k.sb("scores", [128, ncol], F32)
        mslab = k.sb("mslab", [128, 2, RS_ROWS * 128], BF16, 2)
        junk = k.sb("junk", [128, 1024], BF16)
        bis = k.sb("bis", [128, 32], F32)
        idx = k.sb("idx", [128, 4], I32)
        for b in range(cfg.ndec):
            k.DMA("sp", idx.t[0:npg, b:b + 1], dr["page_table"][b, :].rearrange("(n o) -> n o", o=1), [], idx.r)
        idxs = k.sb("idxs", [128, cfg.ndec, nslab], I32)
        for sl in range(nslab):
            k.TS("dve", idxs.t[:, :, sl], idx.t[:, 0:cfg.ndec], float(nslab), float(sl), ALU.mult, ALU.add, idx.r, idxs.r)
        kig8 = k.sb("kig8", [128, 4, RS_ROWS * ID], F32, 4)
        kit8 = k.sb("kit8", [64, 8, RS_ROWS * 128], BF16, 8)
        dn.QITz = k.sb("QITz", [64, 4, IH, 128], BF16)
        k.MS("pool", dn.QITz.t[:], 0.0, dn.QITz.r)
        gkb = [k.sb("gkb%d" % i, [128, RS_ROWS, NKV * HD], BF16) for i in range(2)]
        kg = [k.sb("kg%d" % i, [128, RS_ROWS, NKV * HD], F32) for i in range(2)]
        vg = [k.sb("vg%d" % i, [128, RS_ROWS, NKV * HD], F32) for i in range(2)]
        OT = [dn.banks[4], dn.banks[5]]
        RS = [dn.banks[6], dn.banks[7]]
        for c_ in range(DC):
            k.DMA("sp", dn.x.t[:, c_, :], dr["xsT"][c_ * 128:(c_ + 1) * 128, :], [], [dn.x.r[c_]])
        gi = 0
        for L in range(cfg.depth):
            dn.ffn(c["g_ffn1"][L], dr["w_up1"][L], dr["w_down1"][L])

            def out_k(kh, ap, res, L=L):
                k.DMA("sp", dr["o_ksT"][L, kh * 128:(kh + 1) * 128, :], ap, [res], [], is_out=True)

            def out_v(tt, ap, res, L=L):
                k.DMA("sp", dr["o_vs"][L], ap, [res], [], is_out=True)

            def out_ki(ap, res, L=L):
                k.DMA("sp", dr["o_kisT"][L], ap, [res], [], is_out=True)

            def out_va(t, L=L):
                k.DMA("sp", dr["o_vas"][L], t.t[:], t.r, [], is_out=True)

            dn.mixer_dense(L, own, 0, out_k, out_v, out_ki, out_va=out_va, sample=True)
            assert RS_ROWS == 4 and npg == 128
            W_ = RS_ROWS * 128
            for sl in range(nslab):
                slots = []
                for b in range(cfg.ndec):
                    i8 = gi % 8
                    i4 = gi % 4
                    gi += 1
                    slots.append(i8)
                    src = dr["cache_kidx"][L].rearrange("p (s r) d -> (p s) (r d)", r=RS_ROWS)
                    k.P.dma("pool", lambda e, i4=i4, src=src, b=b, sl=sl: e.indirect_dma_start(
                        out=kig8.t[:, i4, :], out_offset=None, in_=src,
                        in_offset=bass.IndirectOffsetOnAxis(ap=idxs.t[:, b, sl:sl + 1], axis=0)), idxs.r, [kig8.r[i4]])
                    pt = dn.bank((0, 1))
                    for r in range(RS_ROWS):
                        k.TR(pt.t[0:64, r * 128:(r + 1) * 128], kig8.t[:, i4, r * ID:(r + 1) * ID], ident.t[:], [kig8.r[i4]] + ident.r, pt.r)
                    k.CP("act", kit8.t[:, i8, :], pt.t[0:64, 0:W_], pt.r, [kit8.r[i8]])
                for hh in range(IH):
                    pd = dn.bank((2, 3))
                    for b in range(cfg.ndec):
                        k.MM(pd.t[:, 0:W_], dn.QITz.t[:, b, hh, :], kit8.t[:, slots[b], :], b == 0, b == cfg.ndec - 1,
                             dn.QITz.r + [kit8.r[slots[b]]], pd.r)
                    ta, tr = dn.tmp()
                    ti = (dn.tmp_i - 1) % 3
                    ta = dn.tmpf.t[:, ti, :]
                    k.ACT(ta[:, 0:W_], pd.t[:, 0:W_], AF.Relu, pd.r, [tr])
                    sc = scores.t[:, sl * W_:(sl + 1) * W_]
                    if hh == 0:
                        k.TS("dve", sc, ta[:, 0:W_], dn.wtok.t[:, 0, 0:1], None, ALU.mult, None, [tr, dn.wtok.r[0]], scores.r)
                    else:
                        k.STT(sc, ta[:, 0:W_], dn.wtok.t[:, 0, hh:hh + 1], sc, ALU.mult, ALU.add, [tr, dn.wtok.r[0]] + scores.r, scores.r)
            dn.index_scores(own, 0, 128, scores, prange=(0, 128), col0=npast)
            k.TT("dve", scores.t[:, npast:ncol], mnew.t[:], scores.t[:, npast:ncol], ALU.add, scores.r + mnew.r, scores.r)
            dn.bisect_mask(scores, ncol, cfg.ksel_s, None, junk, bis)
            if DBG_STAGE == "sdbg" and L == 0:
                k.DMA("sp", dr["o_dbg_sc"], scores.t[:], scores.r, [], is_out=True)
                k.DMA("sp", dr["o_dbg_bis"], bis.t[:, 0:1], bis.r, [], is_out=True)
            for b in range(cfg.ndec):
                for sl in range(nslab):
                    r0 = sl * RS_ROWS
                    gk, gv = kg[gi % 2], vg[gi % 2]
                    sb_ = slab[gi % 2]
                    gi += 1
                    srck = dr["cache_k"][L].rearrange("p (s r) d -> (p s) (r d)", r=RS_ROWS)
                    srcv = dr["cache_v"][L].rearrange("p (s r) d -> (p s) (r d)", r=RS_ROWS)
                    k.P.dma("pool", lambda e, gk=gk, srck=srck, b=b, sl=sl: e.indirect_dma_start(
                        out=gk.t[:, :, :].rearrange("p r d -> p (r d)"), out_offset=None, in_=srck,
                        in_offset=bass.IndirectOffsetOnAxis(ap=idxs.t[:, b, sl:sl + 1], axis=0)), idxs.r, gk.r)
                    k.P.dma("pool", lambda e, gv=gv, srcv=srcv, b=b, sl=sl: e.indirect_dma_start(
                        out=gv.t[:, :, :].rearrange("p r d -> p (r d)"), out_offset=None, in_=srcv,
                        in_offset=bass.IndirectOffsetOnAxis(ap=idxs.t[:, b, sl:sl + 1], axis=0)), idxs.r, gv.r)
                    for r in range(RS_ROWS):
                        k.CP("dve", sb_.V.t[:, r, :, :].rearrange("p a b -> p (a b)"), gv.t[:, r, :], gv.r, [sb_.Vr[r]])
                    gb_ = gkb[gi % 2]
                    k.CP("dve", gb_.t[:].rearrange("p r d -> p (r d)"), gk.t[:].rearrange("p r d -> p (r d)"), gk.r, gb_.r)
                    pt = dn.bank((0, 1))
                    ptb = pt.t[:].bitcast(BF16)
                    for kh in range(NKV):
                        for r in range(RS_ROWS):
                            c0 = (kh * RS_ROWS + r) * 128
                            k.TR(ptb[:, c0:c0 + 128], gb_.t[:, r, kh * 128:(kh + 1) * 128], c["I4"].t[:, 0, :], gb_.r + c["I4"].r, pt.r)
                    for kh in range(NKV):
                        k.CP("act", sb_.KT.t[:, kh, 0:RS_ROWS * 128], ptb[:, kh * RS_ROWS * 128:(kh + 1) * RS_ROWS * 128], pt.r,
                             [sb_.KTr[r] for r in range(RS_ROWS)])
                    mi = gi % 2
                    mv = Tl(mslab.t[:, mi, :])
                    mv.r = [mslab.r[mi]]
                    W_ = RS_ROWS * 128
                    k.TS("dve", mv.t[:, 0:W_], scores.t[:, sl * W_:(sl + 1) * W_], bis.t[:, 0:1], MB, ALU.is_lt, ALU.mult, scores.r + bis.r, mv.r)
                    dn.attend(sb_, 0, RS_ROWS, mv, 0, 32, 32 * b, sl == 0, False, (OT, RS))
                mv = Tl(mslab.t[:, 0, :])
                mv.r = [mslab.r[0]]
                k.TS("dve", mv.t[:, 0:128], scores.t[:, npast:ncol], bis.t[:, 0:1], MB, ALU.is_lt, ALU.mult, scores.r + bis.r, mv.r)
                dn.attend(own, 0, 1, mv, 0, 32, 32 * b, False, True, (OT, RS))
                dn.attn_finish(0, 32, 32 * b, (OT, RS))
            dn.merge_out(L)
            dn.ffn(c["g_ffn2"][L], dr["w_up2"][L], dr["w_down2"][L])
        for c_ in range(DC):
            k.DMA("sp", dr["o_ysT"][c_ * 128:(c_ + 1) * 128, :], dn.x.t[:, c_, :], [dn.x.r[c_]], [], is_out=True)
        P.emit("s")


def build_nc(cfg, do_prompt=True, do_sample=True):
    nc = bass.Bass("TRN2", target_bir_lowering=False)
    Ld = cfg.depth
    S, ns = cfg.S, cfg.nseq

    def din(name, shape, dt=F32):
        return nc.dram_tensor(name, list(shape), dt, kind="ExternalInput").ap()

    def dout(name, shape):
        return nc.dram_tensor(name, list(shape), F32, kind="ExternalOutput").ap()

    dr = {}
    dr["xT"] = din("xT", [ns, D, S])
    dr["xsT"] = din("xsT", [D, 128])
    dr["cache_k"] = [din("cache_k%d" % L, [cfg.pool, 128, NKV * HD]) for L in range(Ld)]
    dr["cache_v"] = [din("cache_v%d" % L, [cfg.pool, 128, NKV * HD]) for L in range(Ld)]
    dr["cache_kidx"] = [din("cache_kidx%d" % L, [cfg.pool, 128, ID]) for L in range(Ld)]
    dr["page_table"] = din("page_table", [cfg.ndec, cfg.npg], I32)
    for nm in ("g_ffn1", "g_mix", "g_ffn2"):
        dr[nm] = din(nm, [Ld, 128, DC])
    dr["g_q"] = din("g_q", [Ld, 128, 1])
    dr["g_k"] = din("g_k", [Ld, 128, 1])
    dr["g_v"] = din("g_v", [Ld, 128, D])
    dr["b_s"] = din("b_s", [Ld, 1, 8, 128])
    dr["wsT"] = din("wsT", [Ld, 8, 128, 128])
    for nm, kk, nn in (("w_up1", D, 2 * DFF), ("w_down1", DFF, D), ("w_in", D, DIN), ("w_pa", D, D), ("w_pb", D, D),
                       ("w_out", D, D), ("w_up2", D, 2 * DFF), ("w_down2", DFF, D)):
        dr[nm] = din(nm, [Ld, kk, nn])
    for nm, kk, nn in (("w_up1", D, 2 * DFF), ("w_down1", DFF, D), ("w_in", D, DIN), ("w_pa", D, D), ("w_pb", D, D),
                       ("w_out", D, D), ("w_up2", D, 2 * DFF), ("w_down2", DFF, D)):
        dr[nm + "_bf"] = nc.dram_tensor(nm + "_bf", [Ld, kk, nn], BF16, kind="Internal").ap()
    dr["I4"] = din("I4", [128, 4, 128])
    dr["tri"] = din("tri", [128, 128])
    dr["triu"] = din("triu", [128, 128])
    dr["ident"] = din("ident", [128, 128])
    dr["mnew"] = din("mnew", [128, 128])
    dr["o_yT"] = dout("o_yT", [ns, D, S])
    dr["o_kT"] = dout("o_kT", [Ld, ns, NKV * HD, S])
    dr["o_v"] = dout("o_v", [Ld, ns, S, NKV * HD])
    dr["o_kiT"] = dout("o_kiT", [Ld, ns, ID, S])
    dr["o_ysT"] = dout("o_ysT", [D, 128])
    dr["o_ksT"] = dout("o_ksT", [Ld, NKV * HD, 128])
    dr["o_vs"] = dout("o_vs", [Ld, 128, NKV * HD])
    dr["o_kisT"] = dout("o_kisT", [Ld, ID, 128])
    dr["o_vas"] = dout("o_vas", [Ld, 128, D])
    if DBG_STAGE == "sdbg":
        dr["o_dbg_sc"] = dout("o_dbg_sc", [128, cfg.npg * 128 + 128])
        dr["o_dbg_bis"] = dout("o_dbg_bis", [128, 1])
    if do_sample:
        build_sample(nc, dr, cfg)
    if do_prompt:
        build_prompt(nc, dr, cfg)
    return nc


def host_consts():
    I4 = np.tile(np.eye(128, dtype=np.float32)[:, None, :], (1, 4, 1))
    t = np.arange(128)
    tri = np.where(t[None, :] <= t[:, None], 0.0, NEG).astype(np.float32)
    triu = (t[None, :] >= t[:, None]).astype(np.float32)
    ident = np.eye(128, dtype=np.float32)
    b_ = t // 32
    tl = t % 32
    ok = (b_[:, None] == b_[None, :]) & (tl[None, :] <= tl[:, None]) & (tl[None, :] < 4)
    mnew = np.where(ok, 0.0, NEG).astype(np.float32)
    return {"I4": I4, "tri": tri, "triu": triu, "ident": ident, "mnew": mnew}


def make_in_maps(inp, cfg, ncores):
    Ld = cfg.depth
    f = np.float32
    shared = host_consts()
    for nm in ("g_ffn1", "g_mix", "g_ffn2"):
        shared[nm] = np.ascontiguousarray(np.asarray(inp[nm], f).reshape(Ld, DC, 128).transpose(0, 2, 1))
    shared["g_q"] = np.asarray(inp["g_q"], f).reshape(Ld, 128, 1)
    shared["g_k"] = np.asarray(inp["g_k"], f).reshape(Ld, 128, 1)
    shared["g_v"] = np.ascontiguousarray(np.broadcast_to(np.asarray(inp["g_v"], f)[:, None, :], (Ld, 128, D)))
    shared["b_s"] = np.asarray(inp["b_s"], f).reshape(Ld, 1, 8, 128)
    shared["wsT"] = np.ascontiguousarray(np.asarray(inp["w_s"], f).transpose(0, 1, 3, 2))
    for nm in ("w_up1", "w_down1", "w_in", "w_pa", "w_pb", "w_out", "w_up2", "w_down2"):
        shared[nm] = np.asarray(inp[nm], f)
    pool = cfg.pool
    ck = np.asarray(inp["cache_k"], f).reshape(Ld, pool, 128, NKV * HD)
    cv = np.asarray(inp["cache_v"], f).reshape(Ld, pool, 128, NKV * HD)
    cki = np.asarray(inp["cache_kidx"], f)
    for L in range(Ld):
        shared["cache_k%d" % L] = ck[L]
        shared["cache_v%d" % L] = cv[L]
        shared["cache_kidx%d" % L] = cki[L]
    xp = np.asarray(inp["x_prompt"], f)
    xs = np.asarray(inp["x_sample"], f)
    pt = np.asarray(inp["page_table"], np.int32)
    maps = []
    for ci in range(ncores):
        m = dict(shared)
        m["xT"] = np.ascontiguousarray(xp[ci * cfg.nseq:(ci + 1) * cfg.nseq].transpose(0, 2, 1))
        xsT = np.zeros((D, 128), f)
        for b in range(cfg.ndec):
            xsT[:, 32 * b:32 * b + cfg.tdec] = xs[ci * cfg.ndec + b].T
        m["xsT"] = xsT
        m["page_table"] = np.ascontiguousarray(pt[ci * cfg.ndec:(ci + 1) * cfg.ndec])
        maps.append(m)
    return maps


def assemble(results, cfg, ncores):
    Ld = cfg.depth
    ns, S, nd, T = cfg.nseq, cfg.S, cfg.ndec, cfg.tdec
    B, DB = ns * ncores, nd * ncores
    y = np.empty((B, S, D), np.float32)
    nk = np.empty((Ld, B, S, NKV, HD), np.float32)
    nv = np.empty((Ld, B, S, NKV, HD), np.float32)
    nki = np.empty((Ld, B, S, ID), np.float32)
    ys = np.empty((DB, T, D), np.float32)
    ks = np.empty((Ld, DB, T, NKV, HD), np.float32)
    vs = np.empty((Ld, DB, T, NKV, HD), np.float32)
    kis = np.empty((Ld, DB, T, ID), np.float32)
    vas = np.empty((Ld, DB, T, D), np.float32)
    for ci, r in enumerate(results):
        sl = slice(ci * ns, (ci + 1) * ns)
        y[sl] = r["o_yT"].transpose(0, 2, 1)
        nk[:, sl] = r["o_kT"].transpose(0, 1, 3, 2).reshape(Ld, ns, S, NKV, HD)
        nv[:, sl] = r["o_v"].reshape(Ld, ns, S, NKV, HD)
        nki[:, sl] = r["o_kiT"].transpose(0, 1, 3, 2)
        for b in range(nd):
            gb = ci * nd + b
            cols = slice(32 * b, 32 * b + T)
            ys[gb] = r["o_ysT"][:, cols].T
            ks[:, gb] = r["o_ksT"][:, :, cols].transpose(0, 2, 1).reshape(Ld, T, NKV, HD)
            vs[:, gb] = r["o_vs"][:, cols, :].reshape(Ld, T, NKV, HD)
            kis[:, gb] = r["o_kisT"][:, :, cols].transpose(0, 2, 1)
            vas[:, gb] = r["o_vas"][:, cols, :]
    return (y, ys, nk, nv, nki, ks, vs, kis, vas)


def kernel(**inputs):
    cfg = FULL
    nc = build_nc(cfg)
    maps = make_in_maps(inputs, cfg, NCORES)
    res = run_bass_kernel_spmd(nc, maps, core_ids=list(range(NCORES)))
    return assemble(res.results, cfg, NCORES)
```
